# Optimizing a Trainium2 kernel written in Bass

```python
import math
import jax
import jax.numpy as jnp
from jax import lax
import numpy as np

D_MODEL = 2048
BATCH = 8
SEQ = 2048
DEPTH = 2

GRID_W = 64
CTX_LEN = 256

BRANCH_W = D_MODEL // 2
N_BRANCH = 3
HEAD_DIM = 128
N_Q_HEADS = BRANCH_W // HEAD_DIM
N_KV_HEADS = N_Q_HEADS // 4
Q_PER_KV = N_Q_HEADS // N_KV_HEADS
ATTN_W = N_Q_HEADS * HEAD_DIM
KV_W = N_KV_HEADS * HEAD_DIM
WINDOW = 128
BLOCK = 128
ROPE_BASE = 10000.0
ATTN_SCALE = HEAD_DIM ** -0.5
CONV_W = BRANCH_W
CONV_K = 31
SSM_W = BRANCH_W
SSM_GROUP = 16
SSM_GROUPS = SSM_W // SSM_GROUP
SSM_STATE = 64
N_DIR = 2
Q_OFF = 0
K_OFF = Q_OFF + ATTN_W
V_OFF = K_OFF + KV_W
U_OFF = V_OFF + KV_W
CV_OFF = U_OFF + SSM_W
G_OFF = CV_OFF + 2 * CONV_W
N_IN = G_OFF + N_BRANCH * D_MODEL
N_EXPERTS = 16
EC_CAPACITY = 2
EXPERT_FF = D_MODEL // 2
DN_ALPHA = (2 * DEPTH) ** 0.25
DN_BETA = (8 * DEPTH) ** -0.25
LN_EPS = 1e-5
NEG_INF = -1e30

kernel_name = 'hybrid_dit_gqa_conformer_s5_ecmoe'


def _layer_norm(x, g, b):
    xf = x.astype(jnp.float32)
    mu = jnp.mean(xf, axis=-1, keepdims=True)
    var = jnp.mean(jnp.square(xf - mu), axis=-1, keepdims=True)
    y = (xf - mu) * lax.rsqrt(var + LN_EPS)
    return (y * g.astype(jnp.float32) + b.astype(jnp.float32)).astype(x.dtype)


def _grid_angles(n_tok):
    n_rows = n_tok // GRID_W
    rows = jnp.repeat(jnp.arange(n_rows, dtype=jnp.float32), GRID_W)
    cols = jnp.tile(jnp.arange(GRID_W, dtype=jnp.float32), n_rows)
    n_freq = HEAD_DIM // 4
    inv_freq = ROPE_BASE ** (-jnp.arange(n_freq, dtype=jnp.float32) / n_freq)
    return rows[:, None] * inv_freq, cols[:, None] * inv_freq


def _rotate(x, ang):
    m = ang.shape[-1]
    cos = jnp.cos(ang)[:, None, :].astype(x.dtype)
    sin = jnp.sin(ang)[:, None, :].astype(x.dtype)
    x1, x2 = x[..., :m], x[..., m:]
    return jnp.concatenate([x1 * cos - x2 * sin, x1 * sin + x2 * cos], axis=-1)


def _rope_2d(x, ang_row, ang_col):
    half = HEAD_DIM // 2
    return jnp.concatenate([_rotate(x[..., :half], ang_row), _rotate(x[..., half:], ang_col)], axis=-1)


def _sink_column(sink, lead_shape):
    s = sink.astype(jnp.float32).reshape((1,) * (len(lead_shape) - 4) + (N_KV_HEADS, Q_PER_KV, 1, 1))
    return jnp.broadcast_to(s, lead_shape + (1,))


def _latent_attention(q, k, v, k_ctx, v_ctx, sink):
    bsz, n_tok = q.shape[:2]
    n_ctx = k_ctx.shape[1]
    nb = n_tok // BLOCK
    qb = q.reshape(bsz, nb, BLOCK, N_KV_HEADS, Q_PER_KV, HEAD_DIM)

    def windows(t):
        tp = jnp.pad(t, ((0, 0), (BLOCK, BLOCK), (0, 0), (0, 0)))
        tp = tp.reshape(bsz, nb + 2, BLOCK, N_KV_HEADS, HEAD_DIM)
        return jnp.concatenate([tp[:, :-2], tp[:, 1:-1], tp[:, 2:]], axis=2)

    kw, vw = windows(k), windows(v)
    n_loc = 3 * BLOCK
    s_loc = jnp.einsum('bnqhgd,bnkhd->bnhgqk', qb, kw).astype(jnp.float32) * ATTN_SCALE
    qi = jnp.arange(BLOCK)[:, None]
    kj = jnp.arange(n_loc)[None, :]
    rel = kj - BLOCK - qi
    kpos = jnp.arange(nb)[:, None, None] * BLOCK - BLOCK + kj[None]
    valid = (jnp.abs(rel) <= WINDOW)[None] & (kpos >= 0) & (kpos < n_tok)
    s_loc = jnp.where(valid[None, :, None, None], s_loc, NEG_INF)
    s_ctx = jnp.einsum('bnqhgd,bkhd->bnhgqk', qb, k_ctx).astype(jnp.float32) * ATTN_SCALE
    logits = jnp.concatenate([s_loc, s_ctx, _sink_column(sink, s_loc.shape[:-1])], axis=-1)
    p = jax.nn.softmax(logits, axis=-1).astype(v.dtype)
    o = (jnp.einsum('bnhgqk,bnkhd->bnqhgd', p[..., :n_loc], vw)
         + jnp.einsum('bnhgqk,bkhd->bnqhgd', p[..., n_loc:n_loc + n_ctx], v_ctx))
    return o.reshape(bsz, n_tok, ATTN_W)


def _context_attention(q, k, v, sink):
    bsz, n = q.shape[:2]
    qg = q.reshape(bsz, n, N_KV_HEADS, Q_PER_KV, HEAD_DIM)
    s = jnp.einsum('bqhgd,bkhd->bhgqk', qg, k).astype(jnp.float32) * ATTN_SCALE
    logits = jnp.concatenate([s, _sink_column(sink, s.shape[:-1])], axis=-1)
    p = jax.nn.softmax(logits, axis=-1)[..., :-1].astype(v.dtype)
    o = jnp.einsum('bhgqk,bkhd->bqhgd', p, v)
    return o.reshape(bsz, n, ATTN_W)


def _conv_module(z, w_dw, b_dw, ln_g, ln_b):
    a, g = jnp.split(z, 2, axis=-1)
    u = a * jax.nn.sigmoid(g)
    y = lax.conv_general_dilated(u, w_dw[:, None, :], window_strides=(1,),
                                 padding=[(CONV_K // 2, CONV_K // 2)],
                                 dimension_numbers=('NWC', 'WIO', 'NWC'),
                                 feature_group_count=CONV_W)
    y = y + b_dw
    return jax.nn.silu(_layer_norm(y, ln_g, ln_b))


def _s5_discretize(lam_re, lam_im, log_dt):
    lr = jnp.minimum(lam_re, -1e-4)
    dt = jnp.exp(log_dt)[:, None]
    mag = jnp.exp(lr * dt)
    ab_re = mag * jnp.cos(lam_im * dt)
    ab_im = mag * jnp.sin(lam_im * dt)
    den = lr * lr + lam_im * lam_im
    nr = ab_re - 1.0
    co_re = (nr * lr + ab_im * lam_im) / den
    co_im = (ab_im * lr - nr * lam_im) / den
    return ab_re, ab_im, co_re, co_im


def _complex_scan(a_re, a_im, b_re, b_im):
    n_tok = b_re.shape[1]
    a_re = jnp.broadcast_to(a_re, (1, n_tok) + a_re.shape)
    a_im = jnp.broadcast_to(a_im, (1, n_tok) + a_im.shape)

    def combine(e1, e2):
        ar1, ai1, br1, bi1 = e1
        ar2, ai2, br2, bi2 = e2
        return (ar1 * ar2 - ai1 * ai2, ar1 * ai2 + ai1 * ar2,
                ar2 * br1 - ai2 * bi1 + br2, ar2 * bi1 + ai2 * br1 + bi2)

    _, _, s_re, s_im = lax.associative_scan(combine, (a_re, a_im, b_re, b_im), axis=1)
    return s_re, s_im


def _ssm_direction(u, a_re, a_im, bb_re, bb_im, c_re, c_im, s0, reverse, with_y):
    if reverse:
        u = jnp.flip(u, axis=1)
    bu_re = jnp.einsum('blgh,gph->blgp', u, bb_re)
    bu_im = jnp.einsum('blgh,gph->blgp', u, bb_im)
    if s0 is not None:
        s0_re, s0_im = s0
        bu_re = bu_re.at[:, 0].add(a_re * s0_re - a_im * s0_im)
        bu_im = bu_im.at[:, 0].add(a_re * s0_im + a_im * s0_re)
    s_re, s_im = _complex_scan(a_re, a_im, bu_re, bu_im)
    s_fin = (s_re[:, -1], s_im[:, -1])
    if not with_y:
        return None, s_fin
    y = jnp.einsum('blgp,ghp->blgh', s_re, c_re) - jnp.einsum('blgp,ghp->blgh', s_im, c_im)
    if reverse:
        y = jnp.flip(y, axis=1)
    return y, s_fin


def _s5_mixer(u_lat, u_ctx, lam_re, lam_im, log_dt, b_re, b_im, c_re, c_im, d_skip, glu_w, glu_b, ctx_out):
    f32 = jnp.float32

    def groups(u):
        return u.astype(f32).reshape(u.shape[0], u.shape[1], SSM_GROUPS, SSM_GROUP)

    ul, uc = groups(u_lat), groups(u_ctx)
    y_lat = jnp.zeros_like(ul)
    y_ctx = jnp.zeros_like(uc)
    for d in range(N_DIR):
        a_re, a_im, co_re, co_im = _s5_discretize(lam_re[d].astype(f32), lam_im[d].astype(f32), log_dt[d].astype(f32))
        br, bi = b_re[d].astype(f32), b_im[d].astype(f32)
        bb_re = co_re[..., None] * br - co_im[..., None] * bi
        bb_im = co_re[..., None] * bi + co_im[..., None] * br
        cr, ci = c_re[d].astype(f32), c_im[d].astype(f32)
        rev = d == 1
        yc, s_fin = _ssm_direction(uc, a_re, a_im, bb_re, bb_im, cr, ci, None, rev, ctx_out)
        yl, _ = _ssm_direction(ul, a_re, a_im, bb_re, bb_im, cr, ci, s_fin, rev, True)
        y_lat = y_lat + yl
        if ctx_out:
            y_ctx = y_ctx + yc

    def finish(y, u):
        y = y.reshape(u.shape).astype(u.dtype) + d_skip * u
        z = jax.nn.gelu(y)
        return z * jax.nn.sigmoid(z @ glu_w + glu_b)

    return finish(y_lat, u_lat), (finish(y_ctx, u_ctx) if ctx_out else None)


def _gated_merge(branches, gates, w_branch):
    out = gates[..., 0, :] * (branches[0] @ w_branch[0])
    for i in range(1, N_BRANCH):
        out = out + gates[..., i, :] * (branches[i] @ w_branch[i])
    return out


def _token_mixer(h, hc, ang_row, ang_col, w_in, sink, conv_w, conv_b, conv_g, conv_bn,
                 lam_re, lam_im, log_dt, b_re, b_im, c_re, c_im, d_skip, glu_w, glu_b,
                 w_branch, w_out, ctx_out):
    bsz, n_tok, _ = h.shape
    n_ctx = hc.shape[1]
    p = h @ w_in
    base = 0 if ctx_out else K_OFF
    pc = hc @ (w_in if ctx_out else w_in[:, K_OFF:CV_OFF])
    q = _rope_2d(p[..., Q_OFF:K_OFF].reshape(bsz, n_tok, N_Q_HEADS, HEAD_DIM), ang_row, ang_col)
    k = _rope_2d(p[..., K_OFF:V_OFF].reshape(bsz, n_tok, N_KV_HEADS, HEAD_DIM), ang_row, ang_col)
    v = p[..., V_OFF:U_OFF].reshape(bsz, n_tok, N_KV_HEADS, HEAD_DIM)
    kc = pc[..., K_OFF - base:V_OFF - base].reshape(bsz, n_ctx, N_KV_HEADS, HEAD_DIM)
    vc = pc[..., V_OFF - base:U_OFF - base].reshape(bsz, n_ctx, N_KV_HEADS, HEAD_DIM)
    uc = pc[..., U_OFF - base:CV_OFF - base]
    attn = _latent_attention(q, k, v, kc, vc, sink)
    conv = _conv_module(p[..., CV_OFF:G_OFF], conv_w, conv_b, conv_g, conv_bn)
    ssm, ssm_c = _s5_mixer(p[..., U_OFF:CV_OFF], uc, lam_re, lam_im, log_dt, b_re, b_im,
                           c_re, c_im, d_skip, glu_w, glu_b, ctx_out)
    gates = jax.nn.sigmoid(p[..., G_OFF:].reshape(bsz, n_tok, N_BRANCH, D_MODEL))
    out = _gated_merge((attn, conv, ssm), gates, w_branch) @ w_out
    if not ctx_out:
        return out, None
    qc = pc[..., Q_OFF:K_OFF].reshape(bsz, n_ctx, N_Q_HEADS, HEAD_DIM)
    attn_c = _context_attention(qc, kc, vc, sink)
    conv_c = _conv_module(pc[..., CV_OFF:G_OFF], conv_w, conv_b, conv_g, conv_bn)
    gates_c = jax.nn.sigmoid(pc[..., G_OFF:].reshape(bsz, n_ctx, N_BRANCH, D_MODEL))
    out_c = _gated_merge((attn_c, conv_c, ssm_c), gates_c, w_branch) @ w_out
    return out, out_c


def _expert_choice_ffn(h, router_w, w_gate, w_up, w_down):
    bsz, n_tok, _ = h.shape
    cap = EC_CAPACITY * n_tok // N_EXPERTS
    aff = jax.nn.softmax(jnp.einsum('bld,de->ble', h, router_w).astype(jnp.float32), axis=-1)
    top_aff, idx = lax.top_k(jnp.swapaxes(aff, 1, 2), cap)
    xs = jax.vmap(lambda hb, ib: hb[ib])(h, idx)
    g = jnp.einsum('becd,edf->becf', xs, w_gate)
    u = jnp.einsum('becd,edf->becf', xs, w_up)
    y = jnp.einsum('becf,efd->becd', jax.nn.silu(g) * u, w_down) * top_aff[..., None].astype(h.dtype)

    def scatter(yb, ib):
        return jnp.zeros((n_tok, D_MODEL), yb.dtype).at[ib.reshape(-1)].add(yb.reshape(-1, D_MODEL))

    return jax.vmap(scatter)(y, idx)


def setup_inputs(seed: int = 0) -> dict:
    key = jax.random.key(seed)
    ks = iter(jax.random.split(key, 48))
    f32 = jnp.float32

    def nrm(shape, scale):
        return scale * jax.random.normal(next(ks), shape, f32)

    L = DEPTH
    G, P, H = SSM_GROUPS, SSM_STATE, SSM_GROUP
    return {
        'x': nrm((BATCH, SEQ, D_MODEL), 1.0),
        'c': nrm((BATCH, D_MODEL), 1.0),
        'ctx': nrm((BATCH, CTX_LEN, D_MODEL), 1.0),
        'c_ctx': nrm((D_MODEL,), 1.0),
        'w_ada': nrm((L, D_MODEL, 6 * D_MODEL), 0.5 * D_MODEL ** -0.5),
        'b_ada': nrm((L, 6 * D_MODEL), 0.02),
        'w_in': nrm((L, D_MODEL, N_IN), D_MODEL ** -0.5),
        'attn_sink': nrm((L, N_Q_HEADS), 0.5),
        'conv_w': nrm((L, CONV_K, CONV_W), CONV_K ** -0.5),
        'conv_b': nrm((L, CONV_W), 0.02),
        'conv_ln_g': 1.0 + nrm((L, CONV_W), 0.02),
        'conv_ln_b': nrm((L, CONV_W), 0.02),
        'ssm_lam_re': -0.5 + nrm((L, N_DIR, G, P), 0.02),
        'ssm_lam_im': math.pi * jnp.arange(P, dtype=f32) + nrm((L, N_DIR, G, P), 0.02),
        'ssm_log_dt': jax.random.uniform(next(ks), (L, N_DIR, G), f32, math.log(1e-3), math.log(1e-1)),
        'ssm_b_re': nrm((L, N_DIR, G, P, H), (2 * H) ** -0.5),
        'ssm_b_im': nrm((L, N_DIR, G, P, H), (2 * H) ** -0.5),
        'ssm_c_re': nrm((L, N_DIR, G, H, P), 0.5),
        'ssm_c_im': nrm((L, N_DIR, G, H, P), 0.5),
        'ssm_d': nrm((L, SSM_W), 1.0),
        'ssm_glu_w': nrm((L, SSM_W, SSM_W), SSM_W ** -0.5),
        'ssm_glu_b': nrm((L, SSM_W), 0.02),
        'w_branch': nrm((L, N_BRANCH, BRANCH_W, D_MODEL), BRANCH_W ** -0.5),
        'w_out': nrm((L, D_MODEL, D_MODEL), DN_BETA * D_MODEL ** -0.5),
        'ln1_g': 1.0 + nrm((L, D_MODEL), 0.02),
        'ln1_b': nrm((L, D_MODEL), 0.02),
        'ln2_g': 1.0 + nrm((L, D_MODEL), 0.02),
        'ln2_b': nrm((L, D_MODEL), 0.02),
        'router_w': nrm((L, D_MODEL, N_EXPERTS), D_MODEL ** -0.5),
        'exp_w_gate': nrm((L, N_EXPERTS, D_MODEL, EXPERT_FF), D_MODEL ** -0.5),
        'exp_w_up': nrm((L, N_EXPERTS, D_MODEL, EXPERT_FF), D_MODEL ** -0.5),
        'exp_w_down': nrm((L, N_EXPERTS, EXPERT_FF, D_MODEL), DN_BETA * EXPERT_FF ** -0.5),
    }


def reference(x, c, ctx, c_ctx, w_ada, b_ada, w_in, attn_sink, conv_w, conv_b, conv_ln_g, conv_ln_b,
              ssm_lam_re, ssm_lam_im, ssm_log_dt, ssm_b_re, ssm_b_im, ssm_c_re, ssm_c_im, ssm_d,
              ssm_glu_w, ssm_glu_b, w_branch, w_out, ln1_g, ln1_b, ln2_g, ln2_b,
              router_w, exp_w_gate, exp_w_up, exp_w_down):
    bsz = x.shape[0]
    ang_row, ang_col = _grid_angles(x.shape[1])
    sc = jax.nn.silu(c)
    scc = jax.nn.silu(c_ctx)
    xc = ctx
    for l in range(DEPTH):
        last = l == DEPTH - 1
        mod = (sc @ w_ada[l] + b_ada[l]).reshape(bsz, 6, D_MODEL)[:, :, None, :]
        modc = (scc @ w_ada[l] + b_ada[l]).reshape(6, D_MODEL)
        h = x * (1.0 + mod[:, 1]) + mod[:, 0]
        hc = xc * (1.0 + modc[1]) + modc[0]
        m, mc = _token_mixer(h, hc, ang_row, ang_col, w_in[l], attn_sink[l], conv_w[l], conv_b[l],
                             conv_ln_g[l], conv_ln_b[l], ssm_lam_re[l], ssm_lam_im[l], ssm_log_dt[l],
                             ssm_b_re[l], ssm_b_im[l], ssm_c_re[l], ssm_c_im[l], ssm_d[l],
                             ssm_glu_w[l], ssm_glu_b[l], w_branch[l], w_out[l], not last)
        x = _layer_norm(DN_ALPHA * x + mod[:, 2] * m, ln1_g[l], ln1_b[l])
        f = _expert_choice_ffn(x * (1.0 + mod[:, 4]) + mod[:, 3], router_w[l], exp_w_gate[l], exp_w_up[l], exp_w_down[l])
        x = _layer_norm(DN_ALPHA * x + mod[:, 5] * f, ln2_g[l], ln2_b[l])
        if not last:
            xc = _layer_norm(DN_ALPHA * xc + modc[2] * mc, ln1_g[l], ln1_b[l])
            fc = _expert_choice_ffn(xc * (1.0 + modc[4]) + modc[3], router_w[l], exp_w_gate[l], exp_w_up[l], exp_w_down[l])
            xc = _layer_norm(DN_ALPHA * xc + modc[5] * fc, ln2_g[l], ln2_b[l])
    return x
```

```python
import contextlib
import math
import numpy as np
import concourse.bass as bass
import concourse.mybir as mybir
from concourse.bass_utils import run_bass_kernel_spmd

F32 = mybir.dt.float32
F32R = mybir.dt.float32r
I32 = mybir.dt.int32
U32 = mybir.dt.uint32
ALU = mybir.AluOpType
AF = mybir.ActivationFunctionType
AX = mybir.AxisListType

D = 2048
TL = 2048
TC = 256
NCH = D // 128
DEPTH = 2
N_IN = 10752
K_OFF, V_OFF, U_OFF, CV_OFF, G_OFF = 1024, 1280, 1536, 2560, 4608
NQ, NKV, HD = 8, 2, 128
ATTN_SCALE = HD ** -0.5
CONV_K = 31
SSM_G, SSM_P, SSM_H = 64, 64, 16
NE, FF = 16, 1024
DN_ALPHA = (2 * DEPTH) ** 0.25
LN_EPS = 1e-5
NEG = -30000.0


class Buf:
    __slots__ = ("w", "r", "dsem", "name")

    def __init__(self, name=""):
        self.w = None
        self.r = []
        self.dsem = None
        self.name = name


class Eng:
    def __init__(self, name, handle):
        self.name = name
        self.h = handle
        self.sem = None
        self.count = 0
        self.seen = {}


SAME_ENGINE_SYNC = True


class Sched:
    SEM_LIMIT = 30000

    def __init__(self, nc):
        self.nc = nc
        self.e = {
            "pe": Eng("pe", nc.tensor), "act": Eng("act", nc.scalar), "dve": Eng("dve", nc.vector),
            "pool": Eng("pool", nc.gpsimd), "sp": Eng("sp", nc.sync),
        }
        self.semval = {}
        self.free_sems = []
        self.nsem = 0
        self.dma_bufs = []
        for en in self.e.values():
            en.sem = self.new_sem()

    def new_sem(self):
        if self.free_sems:
            return self.free_sems.pop()
        s = self.nc.alloc_semaphore(name=f"s{self.nsem}")
        self.nsem += 1
        self.semval[s] = 0
        return s

    def buf(self, name=""):
        return Buf(name)

    def bufs(self, n, name=""):
        return [Buf(f"{name}{i}") for i in range(n)]

    def _wait(self, en, deps):
        need = {}
        for (sem, val, src) in deps:
            if src == en.name and (src == "pe" or not SAME_ENGINE_SYNC):
                continue
            if en.seen.get(sem, 0) >= val:
                continue
            if need.get(sem, 0) < val:
                need[sem] = val
        for sem, val in need.items():
            en.h.wait_ge(sem, val)
            en.seen[sem] = val

    def _deps(self, reads, writes):
        deps = []
        for b in reads:
            if b.w is not None:
                deps.append(b.w)
        for b in writes:
            if b.w is not None:
                deps.append(b.w)
            deps.extend(b.r)
        return deps

    def _mark(self, ev, reads, writes):
        for b in reads:
            b.r.append(ev)
        for b in writes:
            b.w = ev
            b.r = []

    def op(self, eng, fn, reads=(), writes=()):
        en = self.e[eng]
        self._wait(en, self._deps(reads, writes))
        ins = fn(en.h)
        en.count += 1
        self.semval[en.sem] += 1
        ins.then_inc(en.sem, 1)
        ev = (en.sem, self.semval[en.sem], eng)
        self._mark(ev, reads, writes)
        return ev

    def dma(self, q, out, in_, reads=(), writes=(), owner=None, **kw):
        en = self.e[q]
        self._wait(en, self._deps(reads, writes))
        if owner is None:
            owner = (list(writes) + list(reads))[0]
        if owner.dsem is None:
            owner.dsem = self.new_sem()
            self.dma_bufs.append(owner)
        ins = en.h.dma_start(out=out, in_=in_, **kw)
        self.semval[owner.dsem] += 16
        ins.then_inc(owner.dsem, 16)
        ev = (owner.dsem, self.semval[owner.dsem], "dma")
        self._mark(ev, reads, writes)
        return ev

    def idma(self, out, out_off, in_, in_off, reads=(), writes=(), owner=None, **kw):
        en = self.e["pool"]
        self._wait(en, self._deps(reads, writes))
        if owner.dsem is None:
            owner.dsem = self.new_sem()
            self.dma_bufs.append(owner)
        ins = en.h.indirect_dma_start(out=out, out_offset=out_off, in_=in_, in_offset=in_off, **kw)
        self.semval[owner.dsem] += 16
        ins.then_inc(owner.dsem, 16)
        ev = (owner.dsem, self.semval[owner.dsem], "dma")
        self._mark(ev, reads, writes)
        return ev

    def barrier(self):
        evs = [(en.sem, self.semval[en.sem], "x") for en in self.e.values() if self.semval[en.sem] > 0]
        for b in self.dma_bufs:
            evs.append((b.dsem, self.semval[b.dsem], "dma"))
        for en in self.e.values():
            self._wait(en, [ev for ev in evs if ev[0] is not en.sem])
        for b in self.dma_bufs:
            if self.semval[b.dsem] < self.SEM_LIMIT:
                self.free_sems.append(b.dsem)
            b.dsem = None
        self.dma_bufs = []
        for en in self.e.values():
            if self.semval[en.sem] > self.SEM_LIMIT:
                en.sem = self.new_sem()

    def final_wait(self, ev):
        self._wait(self.e["sp"], [ev])


def col(ap2d, j):
    return ap2d[:, j:j + 1]


class Prog:
    def __init__(self, layers=(0, 1), stop_after=None, debug=(), skip=(), feed=()):
        self.skip = set(skip)
        self.feed = set(feed)
        self.layers = layers
        self.stop_after = stop_after
        self.debug = set(debug)
        self.nc = bass.Bass("TRN2", target_bir_lowering=False)
        self.S = Sched(self.nc)
        self.dram = {}

    def sbt(self, name, shape, dt):
        self._uid = getattr(self, "_uid", 0) + 1
        return self.nc.sbuf_tensor(f"{name}_{self._uid}", shape, dt)

    def din(self, name, shape, dt=F32):
        t = self.nc.dram_tensor(name, list(shape), dt, kind="ExternalInput").ap()
        self.dram[name] = t
        return t

    def dout(self, name, shape, dt=F32):
        t = self.nc.dram_tensor(name, list(shape), dt, kind="ExternalOutput").ap()
        self.dram[name] = t
        return t

    def dscr(self, name, shape, dt=F32):
        kind = "ExternalOutput" if name in self.debug else ("ExternalInput" if name in self.feed else "Internal")
        t = self.nc.dram_tensor(name, list(shape), dt, kind=kind).ap()
        self.dram[name] = t
        return t

    def declare(self):
        L = len(self.layers)
        self.xT = self.din("xT", [D, TL])
        self.ctxT = self.din("ctxT", [D, TC])
        self.ccol = self.din("ccol", [128, NCH, 2])
        self.consts = self.din("consts", [128, NCONST])
        self.rope = self.din("rope", [128, 2 * TL])
        self.cols = self.din("cols", [DEPTH, 128, NCOLS])
        self.w_ada = self.din("w_ada", [DEPTH, D, 6 * D])
        self.w_in = self.din("w_in", [DEPTH, D, N_IN])
        self.outT = self.dout("outT", [D, TL])
        T = TL + TC
        self.T = T
        self.qT = self.dscr("qT", [1024, T])
        self.kT = self.dscr("kT", [256, T])
        self.vtok = self.dscr("vtok", [T, 256])
        self.uT = self.dscr("uT", [1024, T])
        self.aT = self.dscr("aT", [1024, T])
        self.sgT = self.dscr("sgT", [1024, T])
        self.gatesT = self.dscr("gatesT", [3 * D, T])
        self.attnT = self.dscr("attnT", [1024, T])
        self.convT = self.dscr("convT", [1024, T])
        self.ysT = self.dscr("ysT", [1024, T])
        self.ssmT = self.dscr("ssmT", [1024, T])
        self.ssmp = self.din("ssmp", [DEPTH, 2, 128, SSMP_W])
        self.glu_w = self.din("glu_w", [DEPTH, 1024, 1024])
        self.w_branch = self.din("w_branch", [DEPTH, 3, 1024, D])
        self.w_out = self.din("w_out", [DEPTH, D, D])
        self.mergedT = self.dscr("mergedT", [D, T])
        self.x1T = self.dscr("x1T", [D, TL])
        self.xc1T = self.dscr("xc1T", [D, TC])
        self.hmT = self.dscr("hmT", [D, TL])
        self.hmcT = self.dscr("hmcT", [D, TC])
        if "merged" in self.debug:
            self.dbg_merged = self.dout("dbg_merged", [D, T])
        self.router_w = self.din("router_w", [DEPTH, D, NE])
        self.w_gate = self.din("w_gate", [DEPTH, NE, D, FF])
        self.w_up = self.din("w_up", [DEPTH, NE, D, FF])
        self.w_down = self.din("w_down", [DEPTH, NE, FF, D])
        self.hm_tok = self.dscr("hm_tok", [TL, D])
        self.hmc_tok = self.dscr("hmc_tok", [384, D])
        self.f_tok = self.dscr("f_tok", [TL, D])
        self.fc_tok = self.dscr("fc_tok", [384, D])
        if "aff" in self.debug:
            self.dbg_aff = [self.dout("dbg_aff0", [NE, TL]), self.dout("dbg_aff1", [NE, TL])]
            self.dbg_idx = self.dout("dbg_idx", [128, 3 * NE], I32)
            self.dbg_affcol = self.dout("dbg_affcol", [128, 3 * NE])
        self.x2T = self.dscr("x2T", [D, TL])
        self.xc2T = self.dscr("xc2T", [D, TC])

    def cst(self, name, n=None):
        o, w = CONST_OFF[name]
        return self.ct[:, o:o + (w if n is None else n)]

    def lcol(self, name, j=0, n=1):
        o, w = COLS_OFF[name]
        return self.lc[:, o + j:o + j + n]

    def build(self):
        nc, S = self.nc, self.S
        self.declare()
        with contextlib.ExitStack() as es:
            self.es = es
            self.psum = [es.enter_context(nc.psum_tensor(f"ps{i}", [128, 512], F32)) for i in range(8)]
            self.pb = S.bufs(8, "ps")
            self.ct = es.enter_context(self.sbt("ct", [128, NCONST], F32))
            self.lc = es.enter_context(self.sbt("lc", [128, NCOLS], F32))
            self.modS = es.enter_context(self.sbt("modS", [128, 96], F32))
            self.modC = es.enter_context(self.sbt("modC", [128, 96], F32))
            self.op1t = es.enter_context(self.sbt("op1t", [128, 4, NCH], F32))
            self.idxcol = es.enter_context(self.sbt("idxcol", [128, 3, NE], I32))
            self.affcol = es.enter_context(self.sbt("affcol", [128, 3, NE], F32))
            self.b_idx = S.buf("idxcol")
            self.b_ct, self.b_lc, self.b_mod = S.buf("ct"), S.buf("lc"), S.buf("mod")
            blk = es.enter_context(nc.Block())

            @blk.sync
            def _(sync):
                S.dma("sp", self.ct[:], self.consts, writes=[self.b_ct])
                for l in self.layers:
                    self.layer(l)
                    if self.stop_after is not None and self.stop_after[0] == l:
                        break
                S.barrier()
        return nc

    def stop(self, l, name):
        return self.stop_after is not None and self.stop_after == (l, name)

    def layer(self, l):
        S = self.S
        S.dma("sp", self.lc[:], self.cols[l], writes=[self.b_lc])
        self.stage_mod(l)
        S.barrier()
        if self.stop(l, "mod"):
            return
        if "inproj" not in self.skip:
            self.stage_inproj(l)
        S.barrier()
        if self.stop(l, "inproj"):
            return
        if "attn" not in self.skip:
            self.stage_attn(l)
        if self.stop(l, "attn"):
            return
        if "conv" not in self.skip:
            self.stage_conv(l)
        if self.stop(l, "conv"):
            return
        if "ssm" not in self.skip:
            self.stage_ssm(l)
        if self.stop(l, "ssm"):
            return
        if "ssmfin" not in self.skip:
            self.stage_ssmfin(l)
        if self.stop(l, "ssmfin"):
            return
        if "merge" not in self.skip:
            self.stage_merge(l)
        if self.stop(l, "merge"):
            return
        self.stage_route(l)
        if self.stop(l, "route"):
            return
        self.stage_expert(l)
        if self.stop(l, "expert"):
            return
        self.stage_ln2(l)
        if self.stop(l, "ln2"):
            return

    def stage_mod(self, l):
        nc, S = self.nc, self.S
        with contextlib.ExitStack() as es:
            cc = es.enter_context(self.sbt("cc", [128, NCH, 2], F32))
            sc = es.enter_context(self.sbt("sc", [128, NCH, 2], F32R))
            wt = [es.enter_context(self.sbt(f"wada{i}", [128, NCH, 512], F32R)) for i in range(2)]
            b_cc, b_sc = S.buf(), S.buf()
            b_wt = S.bufs(2)
            S.dma("sp", cc[:], self.ccol, writes=[b_cc])
            S.op("act", lambda e: e.activation(out=sc[:], in_=cc[:], func=AF.Silu), reads=[b_cc], writes=[b_sc])
            ps, pb = self.psum[0], self.pb[0]
            for t in range(24):
                w, bw = wt[t % 2], b_wt[t % 2]
                S.dma("pool", w[:], self.w_ada[l, :, t * 512:(t + 1) * 512].rearrange("(k p) n -> p k n", p=128), writes=[bw])
                for jj in range(4):
                    j = t * 4 + jj
                    for k in range(NCH):
                        S.op("pe", lambda e, k=k, jj=jj, j=j, w=w: e.matmul(ps[:, 2 * j:2 * j + 2], w[:, k, jj * 128:(jj + 1) * 128], sc[:, k, :],
                                                                   start=(k == 0), stop=(k == NCH - 1)),
                             reads=[bw, b_sc], writes=[pb])
            psv = ps[:, 0:192].rearrange("p (j two) -> p j two", two=2)
            S.op("dve", lambda e: e.tensor_tensor(out=self.modS[:], in0=psv[:, :, 0], in1=self.lcol("b_ada", 0, 96), op=ALU.add),
                 reads=[pb, self.b_lc], writes=[self.b_mod])
            S.op("dve", lambda e: e.tensor_tensor(out=self.modC[:], in0=psv[:, :, 1], in1=self.lcol("b_ada", 0, 96), op=ALU.add),
                 reads=[pb, self.b_lc], writes=[self.b_mod])
            for i, (m, j) in enumerate([(self.modS, 1), (self.modC, 1), (self.modS, 4), (self.modC, 4)]):
                S.op("dve", lambda e, i=i, m=m, j=j: e.tensor_scalar_add(self.op1t[:, i, :], m[:, j * NCH:(j + 1) * NCH], 1.0),
                     reads=[self.b_mod], writes=[self.b_mod])
            if "mod_dbg" in self.debug:
                dbg = self.dram.get("mod_dbg") or self.dout("mod_dbg", [128, 192])
                S.dma("sp", dbg[:, 0:96], self.modS[:], reads=[self.b_mod])
                S.dma("sp", dbg[:, 96:192], self.modC[:], reads=[self.b_mod])
            S.barrier()

    def modcol(self, which, j, k, ctx=False):
        m = self.modC if ctx else self.modS
        return m[:, j * NCH + k:j * NCH + k + 1]

    def bank(self, sub=None):
        if sub is None:
            i = self._bank = (getattr(self, "_bank", -1) + 1) % 8
        else:
            if not hasattr(self, "_sbank"):
                self._sbank = [-1, -1]
            self._sbank[sub] = (self._sbank[sub] + 1) % 4
            i = sub * 4 + self._sbank[sub]
        return self.psum[i], self.pb[i]

    def stage_inproj(self, l):
        nc, S = self.nc, self.S
        groups = [[(0, 512, False), (512, 512, False)], [(1024, 512, False), (1536, 512, False), (2048, 256, True)]]
        with contextlib.ExitStack() as es:
            hT = es.enter_context(self.sbt("hT", [128, NCH, 1280], F32R))
            b_h = S.buf("hT")
            stg = [es.enter_context(self.sbt(f"stg{i}", [128, 512], F32)) for i in range(4)]
            b_stg = S.bufs(4, "stg")
            wts = [es.enter_context(self.sbt(f"win{i}", [128, NCH, 256], F32R)) for i in range(3)]
            b_w = S.bufs(3, "win")
            outs = [es.enter_context(self.sbt(f"ost{i}", [128, 512], F32)) for i in range(6)]
            b_o = S.bufs(6, "ost")
            qs = [es.enter_context(self.sbt(f"qs{i}", [128, 512], F32R)) for i in range(2)]
            b_qs = S.bufs(2, "qs")
            tmp = [es.enter_context(self.sbt(f"rtmp{i}", [128, 512], F32)) for i in range(2)]
            b_tmp = S.bufs(2, "rtmp")
            pmr = es.enter_context(self.sbt("pmr", [128, 128], F32R))
            b_pmr = S.buf("pmr")
            S.op("act", lambda e: e.activation(out=pmr[:], in_=self.cst("pm"), func=AF.Copy), reads=[self.b_ct], writes=[b_pmr])
            ropet = es.enter_context(self.sbt("ropet", [128, 2 * TL], F32))
            b_rope = S.buf("rope")
            S.dma("sp", ropet[:], self.rope, writes=[b_rope])
            si = oi = wi = ri = 0
            for g, tiles in enumerate(groups):
                base = tiles[0][0]
                for (off, n, isctx) in tiles:
                    for k in range(NCH):
                        st, bs = stg[si % 4], b_stg[si % 4]
                        si += 1
                        src = self.csrc(l)[k * 128:(k + 1) * 128, 0:n] if isctx else self.xsrc(l)[k * 128:(k + 1) * 128, off:off + n]
                        S.dma("sp", st[:, 0:n], src, writes=[bs])
                        S.op("act", lambda e, st=st, n=n, k=k, off=off, isctx=isctx: e.activation(
                            out=hT[:, k, off - base:off - base + n], in_=st[:, 0:n], func=AF.Identity,
                            scale=self.op1(0, k, isctx), bias=self.modcol(0, 0, k, isctx)),
                            reads=[bs, self.b_mod], writes=[b_h])
                for b in range(42):
                    w, bw = wts[wi % 3], b_w[wi % 3]
                    wi += 1
                    S.dma("pool", w[:], self.w_in[l, :, b * 256:(b + 1) * 256].rearrange("(k p) n -> p k n", p=128), writes=[bw])
                    for (off, n, isctx) in tiles:
                        lo = off - base
                        if isctx and l > 0 and not (4 <= b <= 9):
                            continue
                        if b == 5:
                            for tcn in range(n // 128):
                                ps, pb = self.bank()
                                for k in range(NCH):
                                    S.op("pe", lambda e, k=k, tcn=tcn, ps=ps, w=w, lo=lo: e.matmul(
                                        ps[:, 0:256], hT[:, k, lo + tcn * 128:lo + (tcn + 1) * 128], w[:, k, :],
                                        start=(k == 0), stop=(k == NCH - 1)), reads=[bw, b_h], writes=[pb])
                                o, bo = outs[oi % 6], b_o[oi % 6]
                                oi += 1
                                S.op("act", lambda e, o=o, ps=ps: e.activation(out=o[:, 0:256], in_=ps[:, 0:256], func=AF.Copy),
                                     reads=[pb], writes=[bo])
                                S.dma("sp", self.vtok[off + tcn * 128:off + (tcn + 1) * 128, :], o[:, 0:256], reads=[bo])
                            continue
                        for cc in range(2):
                            c = b * 2 + cc
                            ps, pb = self.bank()
                            for k in range(NCH):
                                S.op("pe", lambda e, k=k, cc=cc, ps=ps, w=w, lo=lo, n=n: e.matmul(
                                    ps[:, 0:n], w[:, k, cc * 128:(cc + 1) * 128], hT[:, k, lo:lo + n],
                                    start=(k == 0), stop=(k == NCH - 1)), reads=[bw, b_h], writes=[pb])
                            o, bo = outs[oi % 6], b_o[oi % 6]
                            oi += 1
                            if c < 10:
                                dst = (self.qT[c * 128:(c + 1) * 128] if c < 8 else self.kT[(c - 8) * 128:(c - 7) * 128])[:, off:off + n]
                                if isctx:
                                    S.op("act", lambda e, o=o, ps=ps, n=n: e.activation(out=o[:, 0:n], in_=ps[:, 0:n], func=AF.Copy),
                                         reads=[pb], writes=[bo])
                                else:
                                    q, bq = qs[ri % 2], b_qs[ri % 2]
                                    t2, bt = tmp[ri % 2], b_tmp[ri % 2]
                                    ri += 1
                                    S.op("act", lambda e, q=q, ps=ps, n=n: e.activation(out=q[:, 0:n], in_=ps[:, 0:n], func=AF.Copy),
                                         reads=[pb], writes=[bq])
                                    ps2, pb2 = self.bank()
                                    S.op("pe", lambda e, q=q, ps2=ps2, n=n: e.matmul(ps2[:, 0:n], pmr[:], q[:, 0:n], start=True, stop=True),
                                         reads=[bq, b_pmr], writes=[pb2])
                                    cos = ropet[:, off:off + n]
                                    sin = ropet[:, TL + off:TL + off + n]
                                    S.op("dve", lambda e, o=o, q=q, n=n, cos=cos: e.tensor_tensor(out=o[:, 0:n], in0=q[:, 0:n].bitcast(F32), in1=cos, op=ALU.mult),
                                         reads=[bq, b_rope], writes=[bo])
                                    S.op("dve", lambda e, t2=t2, ps2=ps2, n=n, sin=sin: e.tensor_tensor(out=t2[:, 0:n], in0=ps2[:, 0:n], in1=sin, op=ALU.mult),
                                         reads=[pb2, b_rope], writes=[bt])
                                    S.op("dve", lambda e, o=o, t2=t2, n=n: e.tensor_tensor(out=o[:, 0:n], in0=o[:, 0:n], in1=t2[:, 0:n], op=ALU.add),
                                         reads=[bt, bo], writes=[bo])
                            else:
                                if c < 20:
                                    dst, func = self.uT[(c - 12) * 128:(c - 11) * 128], AF.Copy
                                elif c < 28:
                                    dst, func = self.aT[(c - 20) * 128:(c - 19) * 128], AF.Copy
                                elif c < 36:
                                    dst, func = self.sgT[(c - 28) * 128:(c - 27) * 128], AF.Sigmoid
                                else:
                                    dst, func = self.gatesT[(c - 36) * 128:(c - 35) * 128], AF.Sigmoid
                                dst = dst[:, off:off + n]
                                S.op("act", lambda e, o=o, ps=ps, n=n, func=func: e.activation(out=o[:, 0:n], in_=ps[:, 0:n], func=func),
                                     reads=[pb], writes=[bo])
                            S.dma("sp", dst, o[:, 0:n], reads=[bo])
                S.barrier()

    def xsrc(self, l):
        return self.xT if l == 0 else self.x2T

    def csrc(self, l):
        return self.ctxT if l == 0 else self.xc2T

    def op1(self, which, k, ctx=False):
        i = which * 2 + (1 if ctx else 0)
        return self.op1t[:, i, k:k + 1]


def _layout(items):
    off, o = {}, 0
    for name, w in items:
        off[name] = (o, w)
        o += w
    return off, o


CONST_OFF, NCONST = _layout([("ident", 128), ("pm", 128), ("ones", 128), ("mask", 384), ("iota", 256), ("padidx", 16)])
COLS_OFF, NCOLS = _layout([("b_ada", 96), ("ln1_g", 16), ("ln1_b", 16), ("ln2_g", 16), ("ln2_b", 16), ("conv_b", 8),
                           ("conv_ln_g", 8), ("conv_ln_b", 8), ("ssm_d", 8), ("glu_b", 8), ("conv_w", 8 * CONV_K), ("sink", 8)])


def make_consts():
    c = np.zeros((128, NCONST), np.float32)

    def put(name, a):
        o, w = CONST_OFF[name]
        c[:, o:o + w] = a
    put("ident", np.eye(128, dtype=np.float32))
    pm = np.zeros((128, 128), np.float32)
    for m in range(128):
        if (m % 64) < 32:
            pm[m + 32, m] = -1.0
        else:
            pm[m - 32, m] = 1.0
    put("pm", pm)
    t = np.arange(TL)
    rows = (t // 64).astype(np.float32)
    cols_ = (t % 64).astype(np.float32)
    inv = (np.float32(10000.0) ** (-np.arange(32, dtype=np.float32) / np.float32(32))).astype(np.float32)
    ang = np.zeros((128, TL), np.float32)
    for p in range(128):
        pos = rows if p < 64 else cols_
        ang[p] = (pos * inv[p % 32]).astype(np.float32)
    rope = np.concatenate([np.cos(ang), np.sin(ang)], axis=1).astype(np.float32)
    put("ones", np.ones((128, 128), np.float32))
    qi = np.arange(128)[:, None]
    kj = np.arange(128)[None, :]
    mask = np.zeros((128, 384), np.float32)
    mask[:, 0:128] = np.where(kj >= qi, 0.0, NEG)
    mask[:, 256:384] = np.where(kj <= qi, 0.0, NEG)
    put("mask", mask)
    put("iota", np.broadcast_to(np.arange(256, dtype=np.float32), (128, 256)))
    put("padidx", np.broadcast_to((256 + np.arange(128, dtype=np.float32) - 32)[:, None], (128, 16)))
    return c, rope


def fm(v, nch):
    return np.ascontiguousarray(v.reshape(nch, 128).T)


def make_cols(inp):
    out = np.zeros((DEPTH, 128, NCOLS), np.float32)
    for l in range(DEPTH):
        def put(name, a):
            o, w = COLS_OFF[name]
            out[l, :, o:o + w] = a
        put("b_ada", fm(inp["b_ada"][l], 96))
        for n in ("ln1_g", "ln1_b", "ln2_g", "ln2_b"):
            put(n, fm(inp[n][l], 16))
        for n, src in (("conv_b", "conv_b"), ("conv_ln_g", "conv_ln_g"), ("conv_ln_b", "conv_ln_b"), ("ssm_d", "ssm_d"), ("glu_b", "ssm_glu_b")):
            put(n, fm(inp[src][l], 8))
        cw = inp["conv_w"][l]
        put("conv_w", np.ascontiguousarray(cw.reshape(CONV_K, 8, 128).transpose(2, 1, 0)).reshape(128, 8 * CONV_K))
        put("sink", np.broadcast_to(inp["attn_sink"][l][None, :], (128, 8)))
    return out


def core_inputs(inp, b, consts, cols):
    consts, rope = consts
    cc = np.stack([fm(inp["c"][b], 16), fm(inp["c_ctx"], 16)], axis=-1).astype(np.float32)
    return {
        "xT": np.ascontiguousarray(inp["x"][b].T), "ctxT": np.ascontiguousarray(inp["ctx"][b].T), "ccol": np.ascontiguousarray(cc),
        "consts": consts, "rope": rope, "cols": cols, "w_ada": inp["w_ada"], "w_in": inp["w_in"], "ssmp": make_ssmp(inp),
        "glu_w": inp["ssm_glu_w"], "w_branch": inp["w_branch"], "w_out": inp["w_out"],
        "router_w": inp["router_w"], "w_gate": inp["exp_w_gate"], "w_up": inp["exp_w_up"], "w_down": inp["exp_w_down"],
    }


def _attn_stage(self, l):
    nc, S = self.nc, self.S
    ctx_out = (l == 0)
    with contextlib.ExitStack() as es:
        kT = [es.enter_context(self.sbt(f"akT{i}", [128, self.T], F32R)) for i in range(2)]
        vg = [es.enter_context(self.sbt(f"avg{i}", [128, 18, 128], F32R)) for i in range(2)]
        b_k, b_v = S.bufs(2, "akT"), S.bufs(2, "avg")
        qT = [es.enter_context(self.sbt(f"aqT{i}", [128, self.T], F32R)) for i in range(2)]
        b_q = S.bufs(2, "aqT")
        st = [es.enter_context(self.sbt(f"as{i}", [128, 640], F32)) for i in range(2)]
        b_s = S.bufs(2, "as")
        pn = [es.enter_context(self.sbt(f"apn{i}", [128, 640], F32)) for i in range(2)]
        b_pn = S.bufs(2, "apn")
        pT = [es.enter_context(self.sbt(f"apT{i}", [128, 640], F32R)) for i in range(2)]
        b_pT = S.bufs(2, "apT")
        sm = [es.enter_context(self.sbt(f"asm{i}", [128, 8], F32)) for i in range(2)]
        b_sm = S.bufs(2, "asm")
        ob = [es.enter_context(self.sbt(f"aob{i}", [128, 512], F32)) for i in range(2)]
        b_ob = S.bufs(2, "aob")
        ident = self.cst("ident")
        oic = [0]
        for g in range(NKV):
            S.dma("pool", kT[g][:], self.kT[g * 128:(g + 1) * 128, :], writes=[b_k[g]])
            S.dma("pool", vg[g][:], self.vtok[:, g * 128:(g + 1) * 128].rearrange("(c p) d -> p c d", p=128), writes=[b_v[g]])
        for h in range(NQ):
            g = h // 4
            q, bq = qT[h % 2], b_q[h % 2]
            ncols = self.T if ctx_out else TL
            S.dma("pool", q[:, 0:ncols], self.qT[h * 128:(h + 1) * 128, 0:ncols], writes=[bq])
            sink = self.lcol("sink", h)
            nblk = 18 if ctx_out else 16
            def blk_work(n, ch):
                it = ch
                if n < 16:
                    lo, hi = max(0, (n - 1) * 128), min(TL, (n + 2) * 128)
                    moff = 128 if n == 0 else 0
                    segs = [(lo, hi - lo, moff), (TL, TC, None)]
                else:
                    segs = [(TL, TC, None)]
                W = sum(sg[1] for sg in segs)
                s, bs = st[it % 2], b_s[it % 2]
                p_, bp = pn[it % 2], b_pn[it % 2]
                t_, bt = pT[it % 2], b_pT[it % 2]
                m_, bm = sm[it % 2], b_sm[it % 2]
                qblk = q[:, n * 128:(n + 1) * 128]
                co = 0
                for (k0, kn, moff) in segs:
                    ps, pb = self.bank(ch)
                    S.op("pe", lambda e, ps=ps, k0=k0, kn=kn: e.matmul(ps[:, 0:kn], qblk, kT[g][:, k0:k0 + kn], start=True, stop=True),
                         reads=[bq, b_k[g]], writes=[pb])
                    yield
                    if moff is not None:
                        mask = self.cst("mask")[:, moff:moff + kn]
                        S.op("dve", lambda e, ps=ps, kn=kn, co=co, mask=mask: e.scalar_tensor_tensor(
                            out=s[:, co:co + kn], in0=ps[:, 0:kn], scalar=ATTN_SCALE, in1=mask, op0=ALU.mult, op1=ALU.add),
                            reads=[pb, self.b_ct], writes=[bs])
                        yield
                    else:
                        S.op("act", lambda e, ps=ps, kn=kn, co=co: e.activation(out=s[:, co:co + kn], in_=ps[:, 0:kn], func=AF.Copy, scale=ATTN_SCALE),
                             reads=[pb], writes=[bs])
                        yield
                    co += kn
                S.op("dve", lambda e: e.reduce_max(out=m_[:, 0:1], in_=s[:, 0:W], axis=AX.X), reads=[bs], writes=[bm])
                yield
                S.op("dve", lambda e: e.tensor_scalar(out=m_[:, 1:2], in0=m_[:, 0:1], scalar1=sink, scalar2=-1.0, op0=ALU.max, op1=ALU.mult),
                     reads=[bm, self.b_lc], writes=[bm])
                yield
                S.op("act", lambda e: e.activation(out=s[:, 0:W], in_=s[:, 0:W], func=AF.Exp, bias=m_[:, 1:2], scale=1.0, accum_out=m_[:, 2:3]),
                     reads=[bs, bm], writes=[bs, bm])
                yield
                S.op("act", lambda e: e.activation(out=m_[:, 3:4], in_=sink, func=AF.Exp, bias=m_[:, 1:2], scale=1.0),
                     reads=[bm, self.b_lc], writes=[bm])
                yield
                S.op("dve", lambda e: e.tensor_tensor(out=m_[:, 4:5], in0=m_[:, 2:3], in1=m_[:, 3:4], op=ALU.add), reads=[bm], writes=[bm])
                yield
                S.op("dve", lambda e: e.reciprocal(out=m_[:, 5:6], in_=m_[:, 4:5]), reads=[bm], writes=[bm])
                yield
                S.op("act", lambda e: e.activation(out=p_[:, 0:W], in_=s[:, 0:W], func=AF.Copy, scale=m_[:, 5:6]),
                     reads=[bs, bm], writes=[bp])
                yield
                nb = W // 128
                psT1, pbT1 = self.bank(ch)
                banks = [(psT1, pbT1)]
                if nb > 4:
                    banks.append(self.bank(ch))
                for j in range(nb):
                    psT, pbT = banks[j // 4]
                    S.op("pe", lambda e, psT=psT, j=j: e.transpose(psT[:, (j % 4) * 128:(j % 4 + 1) * 128], p_[:, j * 128:(j + 1) * 128], ident),
                         reads=[bp, self.b_ct], writes=[pbT])
                    yield
                for bi, (psT, pbT) in enumerate(banks):
                    w_ = min(nb - bi * 4, 4) * 128
                    eng = "dve" if bi == 0 else "act"
                    if eng == "dve":
                        S.op("dve", lambda e, psT=psT, bi=bi, w_=w_: e.tensor_copy(out=t_[:, bi * 512:bi * 512 + w_], in_=psT[:, 0:w_]),
                             reads=[pbT], writes=[bt])
                        yield
                    else:
                        S.op("act", lambda e, psT=psT, bi=bi, w_=w_: e.activation(out=t_[:, bi * 512:bi * 512 + w_], in_=psT[:, 0:w_], func=AF.Copy),
                             reads=[pbT], writes=[bt])
                        yield
                psO, pbO = self.bank(ch)
                chunks = []
                for (k0, kn, moff) in segs:
                    chunks += [k0 // 128 + i for i in range(kn // 128)]
                for j, cidx in enumerate(chunks):
                    S.op("pe", lambda e, j=j, cidx=cidx: e.matmul(psO[:, 0:128], vg[g][:, cidx, :], t_[:, j * 128:(j + 1) * 128],
                                                                 start=(j == 0), stop=(j == len(chunks) - 1)),
                         reads=[b_v[g], bt], writes=[pbO])
                    yield
                o, bo = ob[oic[0] % 2], b_ob[oic[0] % 2]
                S.op("act", lambda e: e.activation(out=o[:, (n % 4) * 128:(n % 4 + 1) * 128], in_=psO[:, 0:128], func=AF.Copy),
                     reads=[pbO], writes=[bo])
                yield
            for n0 in range(0, nblk, 2):
                gens = [blk_work(n0 + c, c) for c in range(2)]
                while gens:
                    for g_ in list(gens):
                        try:
                            next(g_)
                        except StopIteration:
                            gens.remove(g_)
                n = n0 + 1
                if n % 4 == 3 or n == nblk - 1:
                    nb0 = (n // 4) * 4
                    wd = (n - nb0 + 1) * 128
                    o, bo = ob[oic[0] % 2], b_ob[oic[0] % 2]
                    S.dma("sp", self.attnT[h * 128:(h + 1) * 128, nb0 * 128:nb0 * 128 + wd], o[:, 0:wd], reads=[bo])
                    oic[0] += 1
        S.barrier()


Prog.stage_attn = _attn_stage


def _conv_stage(self, l):
    nc, S = self.nc, self.S
    seqs = [(0, TL)] + ([(TL, TC)] if l == 0 else [])
    PAD = CONV_K // 2
    with contextlib.ExitStack() as es:
        y = es.enter_context(self.sbt("cy", [128, 8, TL], F32))
        b_y = S.bufs(8, "cy")
        a_s = es.enter_context(self.sbt("ca", [128, TL], F32))
        g_s = es.enter_context(self.sbt("cg", [128, TL], F32))
        b_a, b_g = S.buf("ca"), S.buf("cg")
        ub = [es.enter_context(self.sbt(f"cu{i}", [128, TL + 2 * PAD + 2], F32R)) for i in range(2)]
        b_u = S.bufs(2, "cu")
        dg = [es.enter_context(self.sbt(f"cdg{i}", [128, CONV_K, 128], F32R)) for i in range(2)]
        b_dg = [S.bufs(CONV_K, f"cdg{i}_") for i in range(2)]
        sq = [es.enter_context(self.sbt(f"csq{i}", [128, 512], F32)) for i in range(2)]
        b_sq = S.bufs(2, "csq")
        stt = es.enter_context(self.sbt("cst", [128, 4, 512], F32))
        b_st = S.buf("cst")
        zt = [es.enter_context(self.sbt(f"cz{i}", [128, 512], F32)) for i in range(3)]
        b_z = S.bufs(3, "cz")
        epsc = es.enter_context(self.sbt("ceps", [128, 1], F32))
        b_eps = S.buf("ceps")
        S.op("dve", lambda e: e.memset(epsc[:], LN_EPS), writes=[b_eps])
        ident = self.cst("ident")
        ones = self.cst("ones")
        zi = 0
        for (s0, sl) in seqs:
            tiles = [(t0, min(512, sl - t0)) for t0 in range(0, sl, 512)]
            for k in range(8):
                u, bu = ub[k % 2], b_u[k % 2]
                d_, bd = dg[k % 2], b_dg[k % 2]
                S.dma("sp", a_s[:, 0:sl], self.aT[k * 128:(k + 1) * 128, s0:s0 + sl], writes=[b_a])
                S.dma("sp", g_s[:, 0:sl], self.sgT[k * 128:(k + 1) * 128, s0:s0 + sl], writes=[b_g])
                S.op("dve", lambda e: e.memset(u[:, 0:PAD].bitcast(F32), 0.0), writes=[bu])
                S.op("dve", lambda e: e.memset(u[:, PAD + sl:PAD + sl + PAD].bitcast(F32), 0.0), writes=[bu])
                S.op("dve", lambda e: e.tensor_tensor(out=u[:, PAD:PAD + sl], in0=a_s[:, 0:sl], in1=g_s[:, 0:sl], op=ALU.mult),
                     reads=[b_a, b_g], writes=[bu])
                for tap in range(CONV_K):
                    eng = "act" if tap % 2 == 0 else "dve"
                    wc = self.lcol("conv_w", k * CONV_K + tap)
                    if eng == "act":
                        S.op("act", lambda e, tap=tap, wc=wc: e.activation(out=d_[:, tap, :], in_=ident, func=AF.Copy, scale=wc),
                             reads=[self.b_ct, self.b_lc], writes=[bd[tap]])
                    else:
                        S.op("dve", lambda e, tap=tap, wc=wc: e.tensor_scalar(out=d_[:, tap, :], in0=ident, scalar1=wc, scalar2=None, op0=ALU.mult),
                             reads=[self.b_ct, self.b_lc], writes=[bd[tap]])
                for (t0, n) in tiles:
                    ps, pb = self.bank()
                    for tap in range(CONV_K):
                        S.op("pe", lambda e, tap=tap: e.matmul(ps[:, 0:n], d_[:, tap, :], u[:, t0 + tap:t0 + tap + n],
                                                              start=(tap == 0), stop=(tap == CONV_K - 1)),
                             reads=[bd[tap], bu], writes=[pb])
                    S.op("act", lambda e: e.activation(out=y[:, k, t0:t0 + n], in_=ps[:, 0:n], func=AF.Identity,
                                                       bias=self.lcol("conv_b", k), scale=1.0),
                         reads=[pb, self.b_lc], writes=[b_y[k]])
            for (t0, n) in tiles:
                ps1, pb1 = self.bank()
                ps2, pb2 = self.bank()
                for k in range(8):
                    q_, bq = sq[k % 2], b_sq[k % 2]
                    S.op("act", lambda e: e.activation(out=q_[:, 0:n], in_=y[:, k, t0:t0 + n], func=AF.Square), reads=[b_y[k]], writes=[bq])
                    S.op("pe", lambda e: e.matmul(ps1[:, 0:n], ones, y[:, k, t0:t0 + n], start=(k == 0), stop=(k == 7)),
                         reads=[self.b_ct, b_y[k]], writes=[pb1])
                    S.op("pe", lambda e: e.matmul(ps2[:, 0:n], ones, q_[:, 0:n], start=(k == 0), stop=(k == 7)),
                         reads=[self.b_ct, bq], writes=[pb2])
                mean, rstd, t1 = stt[:, 0, 0:n], stt[:, 1, 0:n], stt[:, 2, 0:n]
                S.op("dve", lambda e: e.tensor_scalar(out=mean, in0=ps1[:, 0:n], scalar1=1.0 / 1024, scalar2=None, op0=ALU.mult),
                     reads=[pb1], writes=[b_st])
                S.op("dve", lambda e: e.tensor_tensor(out=t1, in0=mean, in1=mean, op=ALU.mult), reads=[b_st], writes=[b_st])
                S.op("dve", lambda e: e.scalar_tensor_tensor(out=t1, in0=ps2[:, 0:n], scalar=1.0 / 1024, in1=t1, op0=ALU.mult, op1=ALU.subtract),
                     reads=[pb2, b_st], writes=[b_st])
                S.op("act", lambda e: e.activation(out=t1, in_=t1, func=AF.Ln, bias=epsc[:, 0:1], scale=1.0), reads=[b_st, b_eps], writes=[b_st])
                S.op("act", lambda e: e.activation(out=rstd, in_=t1, func=AF.Exp, scale=-0.5), reads=[b_st], writes=[b_st])
                for k in range(8):
                    S.op("dve", lambda e: e.tensor_tensor(out=y[:, k, t0:t0 + n], in0=y[:, k, t0:t0 + n], in1=mean, op=ALU.subtract),
                         reads=[b_y[k], b_st], writes=[b_y[k]])
                for k in range(8):
                    S.op("dve", lambda e: e.tensor_tensor(out=y[:, k, t0:t0 + n], in0=y[:, k, t0:t0 + n], in1=rstd, op=ALU.mult),
                         reads=[b_y[k], b_st], writes=[b_y[k]])
                for k in range(8):
                    S.op("act", lambda e: e.activation(out=y[:, k, t0:t0 + n], in_=y[:, k, t0:t0 + n], func=AF.Silu,
                                                       scale=self.lcol("conv_ln_g", k), bias=self.lcol("conv_ln_b", k)),
                         reads=[b_y[k], self.b_lc], writes=[b_y[k]])
                for k in range(8):
                    S.dma("sp", self.convT[k * 128:(k + 1) * 128, s0 + t0:s0 + t0 + n], y[:, k, t0:t0 + n], reads=[b_y[k]])
        S.barrier()


Prog.stage_conv = _conv_stage


SSMP_W = 3 * 32 + 4 * 512


def _ssm_stage(self, l):
    nc, S = self.nc, self.S
    T = self.T
    TWO_PI = 2.0 * math.pi
    with contextlib.ExitStack() as es:
        sp = es.enter_context(self.sbt("sp_", [128, 2, SSMP_W], F32))
        b_sp = S.buf("sp")
        S.dma("sp", sp[:], self.ssmp[l].rearrange("d p w -> p d w"), writes=[b_sp])
        pr = es.enter_context(self.sbt("spr", [128, 16, 64], F32))
        pi_ = es.enter_context(self.sbt("spi", [128, 2, 64], I32))
        b_pr = S.buf("spr")
        bb = es.enter_context(self.sbt("sbb", [128, 2, 2, 32, 16], F32))
        b_bb = S.buf("sbb")
        tmpb = es.enter_context(self.sbt("stmpb", [128, 32, 16], F32))

        def P_(i):
            return pr[:, i, :]
        LAMR, LAMI, DT, MAG, TH, SINT, COST, ARE, AIM, CORE, COIM, T0, T1, T2, NSIN, DEN = range(16)
        for d in range(2):
            S.op("dve", lambda e, d=d: e.tensor_scalar(out=pr[:, LAMR, d * 32:(d + 1) * 32], in0=sp[:, d, 0:32], scalar1=-1e-4, scalar2=None, op0=ALU.min),
                 reads=[b_sp], writes=[b_pr])
            S.op("dve", lambda e, d=d: e.tensor_copy(out=pr[:, LAMI, d * 32:(d + 1) * 32], in_=sp[:, d, 32:64]), reads=[b_sp], writes=[b_pr])
            S.op("act", lambda e, d=d: e.activation(out=pr[:, DT, d * 32:(d + 1) * 32], in_=sp[:, d, 64:96], func=AF.Exp), reads=[b_sp], writes=[b_pr])
        W = [b_pr]

        def dve(fn):
            S.op("dve", fn, reads=W, writes=W)

        def act(fn):
            S.op("act", fn, reads=W, writes=W)
        dve(lambda e: e.tensor_tensor(out=P_(T0), in0=P_(LAMR), in1=P_(DT), op=ALU.mult))
        act(lambda e: e.activation(out=P_(MAG), in_=P_(T0), func=AF.Exp))
        dve(lambda e: e.tensor_tensor(out=P_(TH), in0=P_(LAMI), in1=P_(DT), op=ALU.mult))

        def sin_of(dst, shift):
            dve(lambda e: e.tensor_scalar(out=P_(T0), in0=P_(TH), scalar1=shift, scalar2=None, op0=ALU.add))
            dve(lambda e: e.tensor_scalar(out=P_(T1), in0=P_(T0), scalar1=1.0 / TWO_PI, scalar2=64.5, op0=ALU.mult, op1=ALU.add))
            dve(lambda e: e.tensor_copy(out=pi_[:, 0, :], in_=P_(T1)))
            dve(lambda e: e.tensor_copy(out=P_(T1), in_=pi_[:, 0, :]))
            dve(lambda e: e.tensor_scalar(out=P_(T1), in0=P_(T1), scalar1=-64.0, scalar2=-TWO_PI, op0=ALU.add, op1=ALU.mult))
            dve(lambda e: e.tensor_tensor(out=P_(T0), in0=P_(T0), in1=P_(T1), op=ALU.add))
            dve(lambda e: e.tensor_scalar(out=P_(T1), in0=P_(T0), scalar1=-math.pi, scalar2=TWO_PI, op0=ALU.is_lt, op1=ALU.mult))
            dve(lambda e: e.tensor_tensor(out=P_(T0), in0=P_(T0), in1=P_(T1), op=ALU.add))
            dve(lambda e: e.tensor_scalar(out=P_(T1), in0=P_(T0), scalar1=math.pi, scalar2=-TWO_PI, op0=ALU.is_gt, op1=ALU.mult))
            dve(lambda e: e.tensor_tensor(out=P_(T0), in0=P_(T0), in1=P_(T1), op=ALU.add))
            dve(lambda e: e.tensor_scalar(out=P_(T0), in0=P_(T0), scalar1=math.pi, scalar2=-math.pi, op0=ALU.min, op1=ALU.max))
            act(lambda e: e.activation(out=P_(dst), in_=P_(T0), func=AF.Sin))
        sin_of(SINT, 0.0)
        sin_of(COST, math.pi / 2)
        dve(lambda e: e.tensor_tensor(out=P_(ARE), in0=P_(MAG), in1=P_(COST), op=ALU.mult))
        dve(lambda e: e.tensor_tensor(out=P_(AIM), in0=P_(MAG), in1=P_(SINT), op=ALU.mult))
        dve(lambda e: e.tensor_scalar(out=P_(NSIN), in0=P_(SINT), scalar1=-1.0, scalar2=None, op0=ALU.mult))
        dve(lambda e: e.tensor_tensor(out=P_(T0), in0=P_(LAMR), in1=P_(LAMR), op=ALU.mult))
        dve(lambda e: e.tensor_tensor(out=P_(T1), in0=P_(LAMI), in1=P_(LAMI), op=ALU.mult))
        dve(lambda e: e.tensor_tensor(out=P_(DEN), in0=P_(T0), in1=P_(T1), op=ALU.add))
        dve(lambda e: e.reciprocal(out=P_(DEN), in_=P_(DEN)))
        dve(lambda e: e.tensor_scalar(out=P_(T2), in0=P_(ARE), scalar1=-1.0, scalar2=None, op0=ALU.add))
        dve(lambda e: e.tensor_tensor(out=P_(T0), in0=P_(T2), in1=P_(LAMR), op=ALU.mult))
        dve(lambda e: e.tensor_tensor(out=P_(T1), in0=P_(AIM), in1=P_(LAMI), op=ALU.mult))
        dve(lambda e: e.tensor_tensor(out=P_(T0), in0=P_(T0), in1=P_(T1), op=ALU.add))
        dve(lambda e: e.tensor_tensor(out=P_(CORE), in0=P_(T0), in1=P_(DEN), op=ALU.mult))
        dve(lambda e: e.tensor_tensor(out=P_(T0), in0=P_(AIM), in1=P_(LAMR), op=ALU.mult))
        dve(lambda e: e.tensor_tensor(out=P_(T1), in0=P_(T2), in1=P_(LAMI), op=ALU.mult))
        dve(lambda e: e.tensor_tensor(out=P_(T0), in0=P_(T0), in1=P_(T1), op=ALU.subtract))
        dve(lambda e: e.tensor_tensor(out=P_(COIM), in0=P_(T0), in1=P_(DEN), op=ALU.mult))
        W2 = [b_pr, b_sp, b_bb]
        for d in range(2):
            bre = sp[:, d, 96:96 + 512].rearrange("p (c h) -> p c h", h=16)
            bim = sp[:, d, 96 + 512:96 + 1024].rearrange("p (c h) -> p c h", h=16)
            cor = pr[:, CORE, d * 32:(d + 1) * 32].unsqueeze(2).to_broadcast([128, 32, 16])
            coi = pr[:, COIM, d * 32:(d + 1) * 32].unsqueeze(2).to_broadcast([128, 32, 16])
            S.op("dve", lambda e: e.tensor_tensor(out=bb[:, d, 0], in0=bre, in1=cor, op=ALU.mult), reads=W2, writes=W2)
            S.op("dve", lambda e: e.tensor_tensor(out=tmpb[:], in0=bim, in1=coi, op=ALU.mult), reads=W2, writes=W2)
            S.op("dve", lambda e: e.tensor_tensor(out=bb[:, d, 0], in0=bb[:, d, 0], in1=tmpb[:], op=ALU.subtract), reads=W2, writes=W2)
            S.op("dve", lambda e: e.tensor_tensor(out=bb[:, d, 1], in0=bim, in1=cor, op=ALU.mult), reads=W2, writes=W2)
            S.op("dve", lambda e: e.tensor_tensor(out=tmpb[:], in0=bre, in1=coi, op=ALU.mult), reads=W2, writes=W2)
            S.op("dve", lambda e: e.tensor_tensor(out=bb[:, d, 1], in0=bb[:, d, 1], in1=tmpb[:], op=ALU.add), reads=W2, writes=W2)
        if "ssm_dbg" in self.debug:
            dbg = self.dout("ssm_dbg", [128, 16 * 64 + 2 * 2 * 512])
            S.dma("sp", dbg[:, 0:1024], pr[:].rearrange("p a b -> p (a b)"), reads=W2)
            S.dma("sp", dbg[:, 1024:], bb[:].rearrange("p a b c h -> p (a b c h)"), reads=W2)

        NT = T + 1
        Rt = [es.enter_context(self.sbt(f"sR{i}", [128, NT], F32)) for i in range(2)]
        b_R = S.buf("sR")
        bp = [es.enter_context(self.sbt(f"sbp{i}", [128, T], F32)) for i in range(2)]
        b_bp = S.buf("sbp")
        tt = [es.enter_context(self.sbt(f"stt{i}", [128, T], F32)) for i in range(2)]
        b_tt = S.buf("stt")
        zz = [es.enter_context(self.sbt(f"szz{i}", [128, T], F32)) for i in range(2)]
        b_zz = S.buf("szz")
        ss = [es.enter_context(self.sbt(f"sss{i}", [128, T], F32R)) for i in range(2)]
        b_ss = S.buf("sss")
        uu = [es.enter_context(self.sbt(f"suu{i}", [128, T], F32R)) for i in range(2)]
        b_uu = S.bufs(2, "suu")
        yacc = es.enter_context(self.sbt("syacc", [128, T], F32))
        b_y = S.buf("syacc")
        wsm = [es.enter_context(self.sbt(f"swsm{i}", [128, 2, 128], F32)) for i in range(4)]
        b_wsm = S.bufs(4, "swsm")
        wc = [es.enter_context(self.sbt(f"swc{i}", [128, 2, 128], F32R)) for i in range(4)]
        b_wc = S.bufs(4, "swc")
        lb = [es.enter_context(self.sbt(f"slb{i}", [128, 2, 128], F32R)) for i in range(2)]
        b_lb = S.bufs(2, "slb")
        for i in range(4):
            S.op("dve", lambda e, i=i: e.memset(wsm[i][:], 0.0), writes=[b_wsm[i]])
            S.op("dve", lambda e, i=i: e.memset(wc[i][:].bitcast(F32), 0.0), writes=[b_wc[i]])
        ident = self.cst("ident")
        ttiles = [[(0, 256)] + [(256 + 512 * i, 512) for i in range(4)], [(512 * i, 512) for i in range(4)] + [(2048, 256)]]
        it = 0
        for q in range(8):
            S.dma("pool", uu[0][:, 0:TC], self.uT[q * 128:(q + 1) * 128, TL:T], writes=[b_uu[0]])
            S.dma("pool", uu[0][:, TC:T], self.uT[q * 128:(q + 1) * 128, 0:TL], writes=[b_uu[0]])
            S.dma("pool", uu[1][:, :], self.uT[q * 128:(q + 1) * 128, :], writes=[b_uu[1]])
            first = True
            for d in range(2):
                rev = (d == 1)

                def tv(ap2):
                    return ap2[:, ::-1] if rev else ap2
                for j in range(4):
                    sc = 4 * q + j
                    cidx = d * 32 + sc
                    c0 = 32 * j
                    w_, bw = wsm[j], b_wsm[j]
                    c_, bc = wc[j], b_wc[j]
                    l_, bl = lb[it % 2], b_lb[it % 2]
                    it += 1
                    for ri in range(2):
                        for half in range(2):
                            rows = slice(64 * half, 64 * half + 64)
                            cols = slice(c0 + 16 * half, c0 + 16 * half + 16)
                            S.op("dve", lambda e: e.tensor_copy(out=w_[rows, ri, cols], in_=bb[rows, d, ri, sc, :]), reads=[b_bb], writes=[bw])
                            csrc = sp[rows, d, 96 + 1024 + ri * 512 + sc * 16:96 + 1024 + ri * 512 + sc * 16 + 16]
                            if ri == 0:
                                S.op("act", lambda e: e.activation(out=c_[rows, ri, cols], in_=csrc, func=AF.Copy), reads=[b_sp], writes=[bc])
                            else:
                                S.op("act", lambda e: e.activation(out=c_[rows, ri, cols], in_=csrc, func=AF.Copy, scale=-1.0), reads=[b_sp], writes=[bc])
                        psT, pbT = self.bank()
                        S.op("pe", lambda e: e.transpose(psT[:, 0:128], w_[:, ri, :], ident), reads=[bw, self.b_ct], writes=[pbT])
                        S.op("act", lambda e: e.activation(out=l_[:, ri, :], in_=psT[:, 0:128], func=AF.Copy), reads=[pbT], writes=[bl])
                    S.op("dve", lambda e: e.memset(Rt[0][:, 0:1], 1.0), reads=[b_R], writes=[b_R])
                    S.op("dve", lambda e: e.memset(Rt[1][:, 0:1], 0.0), reads=[b_R], writes=[b_R])
                    S.op("dve", lambda e: e.tensor_copy(out=Rt[0][:, 1:2], in_=pr[:, COST, cidx:cidx + 1]), reads=[b_R, b_pr], writes=[b_R])
                    S.op("dve", lambda e: e.tensor_copy(out=Rt[1][:, 1:2], in_=pr[:, NSIN, cidx:cidx + 1]), reads=[b_R, b_pr], writes=[b_R])
                    m = 1
                    while m < T:
                        n_ = min(m, T - m)
                        sr, si = Rt[0][:, m:m + 1], Rt[1][:, m:m + 1]
                        src_r, src_i = Rt[0][:, 1:1 + n_], Rt[1][:, 1:1 + n_]
                        dst_r, dst_i = Rt[0][:, 1 + m:1 + m + n_], Rt[1][:, 1 + m:1 + m + n_]
                        tmp = tt[0][:, 0:n_]
                        RW = [b_R, b_tt]
                        S.op("dve", lambda e: e.tensor_scalar(out=tmp, in0=src_i, scalar1=si, scalar2=None, op0=ALU.mult), reads=RW, writes=RW)
                        S.op("dve", lambda e: e.scalar_tensor_tensor(out=dst_r, in0=src_r, scalar=sr, in1=tmp, op0=ALU.mult, op1=ALU.subtract), reads=RW, writes=RW)
                        S.op("dve", lambda e: e.tensor_scalar(out=tmp, in0=src_i, scalar1=sr, scalar2=None, op0=ALU.mult), reads=RW, writes=RW)
                        S.op("dve", lambda e: e.scalar_tensor_tensor(out=dst_i, in0=src_r, scalar=si, in1=tmp, op0=ALU.mult, op1=ALU.add), reads=RW, writes=RW)
                        m += n_
                    Rr, Ri = tv(Rt[0][:, 0:T]), tv(Rt[1][:, 0:T])
                    if rev:
                        Rr, Ri = Rt[0][:, 0:T][:, ::-1], Rt[1][:, 0:T][:, ::-1]
                    else:
                        Rr, Ri = Rt[0][:, 0:T], Rt[1][:, 0:T]
                    for ri in range(2):
                        for (t0, n) in ttiles[d]:
                            ps, pb = self.bank()
                            S.op("pe", lambda e: e.matmul(ps[:, 0:n], l_[:, ri, :], uu[d][:, t0:t0 + n], start=True, stop=True),
                                 reads=[bl, b_uu[d]], writes=[pb])
                            sl = slice(t0, t0 + n)
                            A = [pb, b_R, b_bp, b_tt]
                            if ri == 0:
                                S.op("dve", lambda e: e.tensor_tensor(out=bp[0][:, sl], in0=ps[:, 0:n], in1=Rr[:, sl], op=ALU.mult), reads=A, writes=[b_bp])
                                S.op("dve", lambda e: e.tensor_tensor(out=bp[1][:, sl], in0=ps[:, 0:n], in1=Ri[:, sl], op=ALU.mult), reads=A, writes=[b_bp])
                            else:
                                S.op("dve", lambda e: e.tensor_tensor(out=tt[0][:, sl], in0=ps[:, 0:n], in1=Ri[:, sl], op=ALU.mult), reads=A, writes=[b_tt])
                                S.op("dve", lambda e: e.tensor_tensor(out=tt[1][:, sl], in0=ps[:, 0:n], in1=Rr[:, sl], op=ALU.mult), reads=A, writes=[b_tt])
                    A = [b_bp, b_tt]
                    S.op("dve", lambda e: e.tensor_tensor(out=bp[0][:], in0=bp[0][:], in1=tt[0][:], op=ALU.subtract), reads=A, writes=[b_bp])
                    S.op("dve", lambda e: e.tensor_tensor(out=bp[1][:], in0=bp[1][:], in1=tt[1][:], op=ALU.add), reads=A, writes=[b_bp])
                    dec = pr[:, MAG, cidx:cidx + 1].to_broadcast([128, T])
                    for ri in range(2):
                        S.op("dve", lambda e: e.tensor_tensor_scan(tv(zz[ri][:]), dec, tv(bp[ri][:]), 0.0, ALU.mult, ALU.add),
                             reads=[b_bp, b_pr], writes=[b_zz])
                    A = [b_zz, b_R, b_tt, b_ss]
                    S.op("dve", lambda e: e.tensor_tensor(out=tt[0][:], in0=zz[1][:], in1=Ri, op=ALU.mult), reads=A, writes=[b_tt])
                    S.op("dve", lambda e: e.tensor_tensor(out=tt[1][:], in0=zz[0][:], in1=Ri, op=ALU.mult), reads=A, writes=[b_tt])
                    S.op("dve", lambda e: e.tensor_tensor(out=bp[0][:], in0=zz[0][:], in1=Rr, op=ALU.mult), reads=A + [b_bp], writes=[b_bp])
                    S.op("dve", lambda e: e.tensor_tensor(out=bp[1][:], in0=zz[1][:], in1=Rr, op=ALU.mult), reads=A + [b_bp], writes=[b_bp])
                    S.op("dve", lambda e: e.tensor_tensor(out=ss[0][:], in0=bp[0][:], in1=tt[0][:], op=ALU.add), reads=[b_bp, b_tt], writes=[b_ss])
                    S.op("dve", lambda e: e.tensor_tensor(out=ss[1][:], in0=bp[1][:], in1=tt[1][:], op=ALU.subtract), reads=[b_bp, b_tt], writes=[b_ss])
                    for (t0, n) in ttiles[d]:
                        ps, pb = self.bank()
                        for ri in range(2):
                            S.op("pe", lambda e: e.matmul(ps[:, 0:n], c_[:, ri, :], ss[ri][:, t0:t0 + n], start=(ri == 0), stop=(ri == 1)),
                                 reads=[bc, b_ss], writes=[pb])
                        if d == 0:
                            m0 = TL + t0 if t0 < TC else t0 - TC
                        else:
                            m0 = t0
                        if first:
                            S.op("act", lambda e: e.activation(out=yacc[:, m0:m0 + n], in_=ps[:, 0:n], func=AF.Copy), reads=[pb], writes=[b_y])
                        else:
                            S.op("dve", lambda e: e.tensor_tensor(out=yacc[:, m0:m0 + n], in0=ps[:, 0:n], in1=yacc[:, m0:m0 + n], op=ALU.add),
                                 reads=[pb, b_y], writes=[b_y])
                    first = False
            S.dma("sp", self.ysT[q * 128:(q + 1) * 128, :], yacc[:], reads=[b_y])
        S.barrier()


Prog.stage_ssm = _ssm_stage


def make_ssmp(inp):
    out = np.zeros((DEPTH, 2, 128, SSMP_W), np.float32)

    def sm(a, w):
        return a.reshape(32, 128, w).transpose(1, 0, 2).reshape(128, 32 * w)
    for l in range(DEPTH):
        for d in range(2):
            o = out[l, d]
            o[:, 0:32] = sm(inp["ssm_lam_re"][l, d].reshape(4096, 1), 1)
            o[:, 32:64] = sm(inp["ssm_lam_im"][l, d].reshape(4096, 1), 1)
            o[:, 64:96] = sm(np.repeat(inp["ssm_log_dt"][l, d], 64).reshape(4096, 1), 1)
            o[:, 96:96 + 512] = sm(inp["ssm_b_re"][l, d].reshape(4096, 16), 16)
            o[:, 96 + 512:96 + 1024] = sm(inp["ssm_b_im"][l, d].reshape(4096, 16), 16)
            o[:, 96 + 1024:96 + 1536] = sm(inp["ssm_c_re"][l, d].transpose(0, 2, 1).reshape(4096, 16), 16)
            o[:, 96 + 1536:96 + 2048] = sm(inp["ssm_c_im"][l, d].transpose(0, 2, 1).reshape(4096, 16), 16)
    return out


def _tok_tiles(l):
    t = [(i * 512, 512, False) for i in range(4)]
    if l == 0:
        t.append((TL, TC, True))
    return t


def _ssmfin_stage(self, l):
    nc, S = self.nc, self.S
    with contextlib.ExitStack() as es:
        gw = es.enter_context(self.sbt("fgw", [128, 8, 1024], F32R))
        b_gw = S.buf("fgw")
        S.dma("pool", gw[:], self.glu_w[l].rearrange("(k p) n -> p k n", p=128), writes=[b_gw])
        ys = [es.enter_context(self.sbt(f"fys{i}", [128, 8, 512], F32)) for i in range(2)]
        us = [es.enter_context(self.sbt(f"fus{i}", [128, 8, 512], F32)) for i in range(2)]
        zT = [es.enter_context(self.sbt(f"fz{i}", [128, 8, 512], F32R)) for i in range(2)]
        b_ys, b_us, b_z = S.bufs(2, "fys"), S.bufs(2, "fus"), S.bufs(2, "fz")
        sg = [es.enter_context(self.sbt(f"fsg{i}", [128, 512], F32)) for i in range(3)]
        b_sg = S.bufs(3, "fsg")
        oi = 0
        for ti, (t0, n, isctx) in enumerate(_tok_tiles(l)):
            y_, u_, z_ = ys[ti % 2], us[ti % 2], zT[ti % 2]
            by, bu, bz = b_ys[ti % 2], b_us[ti % 2], b_z[ti % 2]
            S.dma("sp", y_[:, :, 0:n], self.ysT[:, t0:t0 + n].rearrange("(k p) t -> p k t", p=128), writes=[by])
            S.dma("sp", u_[:, :, 0:n], self.uT[:, t0:t0 + n].rearrange("(k p) t -> p k t", p=128), writes=[bu])
            for k in range(8):
                S.op("dve", lambda e: e.scalar_tensor_tensor(out=y_[:, k, 0:n], in0=u_[:, k, 0:n], scalar=self.lcol("ssm_d", k), in1=y_[:, k, 0:n],
                                                             op0=ALU.mult, op1=ALU.add), reads=[by, bu, self.b_lc], writes=[by])
                S.op("act", lambda e: e.activation(out=z_[:, k, 0:n], in_=y_[:, k, 0:n], func=AF.Gelu), reads=[by], writes=[bz])
            for m in range(8):
                ps, pb = self.bank()
                for k in range(8):
                    S.op("pe", lambda e: e.matmul(ps[:, 0:n], gw[:, k, m * 128:(m + 1) * 128], z_[:, k, 0:n], start=(k == 0), stop=(k == 7)),
                         reads=[b_gw, bz], writes=[pb])
                s_, bs = sg[oi % 3], b_sg[oi % 3]
                oi += 1
                S.op("act", lambda e: e.activation(out=s_[:, 0:n], in_=ps[:, 0:n], func=AF.Sigmoid, bias=self.lcol("glu_b", m), scale=1.0),
                     reads=[pb, self.b_lc], writes=[bs])
                S.op("dve", lambda e: e.tensor_tensor(out=s_[:, 0:n], in0=s_[:, 0:n], in1=z_[:, m, 0:n].bitcast(F32), op=ALU.mult),
                     reads=[bs, bz], writes=[bs])
                S.dma("sp", self.ssmT[m * 128:(m + 1) * 128, t0:t0 + n], s_[:, 0:n], reads=[bs])
        S.barrier()


def _ln_tile(self, es_bufs, r, b_r, n, gname, bname, outs):
    S = self.S
    stt, b_st, sq, b_sq, zt, b_z, epsc, b_eps = es_bufs
    ones = self.cst("ones")
    ps1, pb1 = self.bank()
    ps2, pb2 = self.bank()
    for k in range(NCH):
        q_, bq = sq[k % 2], b_sq[k % 2]
        S.op("act", lambda e: e.activation(out=q_[:, 0:n], in_=r[:, k, 0:n], func=AF.Square), reads=[b_r[k]], writes=[bq])
        S.op("pe", lambda e: e.matmul(ps1[:, 0:n], ones, r[:, k, 0:n], start=(k == 0), stop=(k == NCH - 1)), reads=[self.b_ct, b_r[k]], writes=[pb1])
        S.op("pe", lambda e: e.matmul(ps2[:, 0:n], ones, q_[:, 0:n], start=(k == 0), stop=(k == NCH - 1)), reads=[self.b_ct, bq], writes=[pb2])
    mean, rstd, t1 = stt[:, 0, 0:n], stt[:, 1, 0:n], stt[:, 2, 0:n]
    S.op("dve", lambda e: e.tensor_scalar(out=mean, in0=ps1[:, 0:n], scalar1=1.0 / D, scalar2=None, op0=ALU.mult), reads=[pb1], writes=[b_st])
    S.op("dve", lambda e: e.tensor_tensor(out=t1, in0=mean, in1=mean, op=ALU.mult), reads=[b_st], writes=[b_st])
    S.op("dve", lambda e: e.scalar_tensor_tensor(out=t1, in0=ps2[:, 0:n], scalar=1.0 / D, in1=t1, op0=ALU.mult, op1=ALU.subtract),
         reads=[pb2, b_st], writes=[b_st])
    S.op("act", lambda e: e.activation(out=t1, in_=t1, func=AF.Ln, bias=epsc[:, 0:1], scale=1.0), reads=[b_st, b_eps], writes=[b_st])
    S.op("act", lambda e: e.activation(out=rstd, in_=t1, func=AF.Exp, scale=-0.5), reads=[b_st], writes=[b_st])
    for k in range(NCH):
        S.op("dve", lambda e: e.tensor_tensor(out=r[:, k, 0:n], in0=r[:, k, 0:n], in1=mean, op=ALU.subtract), reads=[b_st, b_r[k]], writes=[b_r[k]])
    for k in range(NCH):
        S.op("dve", lambda e: e.tensor_tensor(out=r[:, k, 0:n], in0=r[:, k, 0:n], in1=rstd, op=ALU.mult), reads=[b_st, b_r[k]], writes=[b_r[k]])
    zi = 0
    for k in range(NCH):
        S.op("act", lambda e: e.activation(out=r[:, k, 0:n], in_=r[:, k, 0:n], func=AF.Identity, scale=self.lcol(gname, k), bias=self.lcol(bname, k)),
             reads=[b_r[k], self.b_lc], writes=[b_r[k]])
    for k in range(NCH):
        for (dst_fn, sc_fn, bi_fn) in outs:
            if sc_fn is None:
                S.dma("sp", dst_fn(k), r[:, k, 0:n], reads=[b_r[k]])
            else:
                z, bz = zt[zi % 3], b_z[zi % 3]
                zi += 1
                S.op("act", lambda e: e.activation(out=z[:, 0:n], in_=r[:, k, 0:n], func=AF.Identity, scale=sc_fn(k), bias=bi_fn(k)),
                     reads=[b_r[k], self.b_mod], writes=[bz])
                S.dma("sp", dst_fn(k), z[:, 0:n], reads=[bz])


def _ln_bufs(self, es):
    nc, S = self.nc, self.S
    stt = es.enter_context(self.sbt("lst", [128, 3, 512], F32))
    sq = [es.enter_context(self.sbt(f"lsq{i}", [128, 512], F32)) for i in range(2)]
    zt = [es.enter_context(self.sbt(f"lz{i}", [128, 512], F32)) for i in range(3)]
    epsc = es.enter_context(self.sbt("leps", [128, 1], F32))
    b_eps = S.buf("leps")
    S.op("dve", lambda e: e.memset(epsc[:], LN_EPS), writes=[b_eps])
    return (stt, S.buf("lst"), sq, S.bufs(2, "lsq"), zt, S.bufs(3, "lz"), epsc, b_eps)


def _merge_stage(self, l):
    nc, S = self.nc, self.S
    NWB = 4
    with contextlib.ExitStack() as es:
        br = [es.enter_context(self.sbt(f"mbr{i}", [128, 8, 512], F32R)) for i in range(3)]
        b_br = S.bufs(3, "mbr")
        wb = [es.enter_context(self.sbt(f"mwb{i}", [128, 3, 8, 128], F32R)) for i in range(NWB)]
        b_wb = S.bufs(NWB, "mwb")
        gt = [es.enter_context(self.sbt(f"mgt{i}", [128, 3, 512], F32)) for i in range(3)]
        b_gt = S.bufs(3, "mgt")
        tm = [es.enter_context(self.sbt(f"mtm{i}", [128, 512], F32)) for i in range(3)]
        b_tm = S.bufs(3, "mtm")
        srcs = [self.attnT, self.convT, self.ssmT]
        wi = 0
        for (t0, n, isctx) in _tok_tiles(l):
            for i in range(3):
                S.dma("pool", br[i][:, :, 0:n], srcs[i][:, t0:t0 + n].rearrange("(k p) t -> p k t", p=128), writes=[b_br[i]])
            for m in range(NCH):
                w, bw = wb[wi % NWB], b_wb[wi % NWB]
                g_, bg = gt[wi % 3], b_gt[wi % 3]
                t_, bt = tm[wi % 3], b_tm[wi % 3]
                wi += 1
                S.dma("pool", w[:], self.w_branch[l, :, :, m * 128:(m + 1) * 128].rearrange("i (k p) n -> p i k n", p=128), writes=[bw])
                S.dma("sp", g_[:, :, 0:n], self.gatesT[:, t0:t0 + n].rearrange("(i c p) t -> p i c t", i=3, p=128)[:, :, m, :], writes=[bg])
                pss = []
                for i in range(3):
                    ps, pb = self.bank()
                    pss.append((ps, pb))
                    for k in range(8):
                        S.op("pe", lambda e: e.matmul(ps[:, 0:n], w[:, i, k, :], br[i][:, k, 0:n], start=(k == 0), stop=(k == 7)),
                             reads=[bw, b_br[i]], writes=[pb])
                S.op("dve", lambda e: e.tensor_tensor(out=t_[:, 0:n], in0=pss[0][0][:, 0:n], in1=g_[:, 0, 0:n], op=ALU.mult), reads=[pss[0][1], bg], writes=[bt])
                S.op("dve", lambda e: e.tensor_tensor(out=g_[:, 1, 0:n], in0=pss[1][0][:, 0:n], in1=g_[:, 1, 0:n], op=ALU.mult), reads=[pss[1][1], bg], writes=[bg])
                S.op("dve", lambda e: e.tensor_tensor(out=g_[:, 2, 0:n], in0=pss[2][0][:, 0:n], in1=g_[:, 2, 0:n], op=ALU.mult), reads=[pss[2][1], bg], writes=[bg])
                S.op("dve", lambda e: e.tensor_tensor(out=t_[:, 0:n], in0=t_[:, 0:n], in1=g_[:, 1, 0:n], op=ALU.add), reads=[bg, bt], writes=[bt])
                S.op("dve", lambda e: e.tensor_tensor(out=t_[:, 0:n], in0=t_[:, 0:n], in1=g_[:, 2, 0:n], op=ALU.add), reads=[bg, bt], writes=[bt])
                S.dma("sp", self.mergedT[m * 128:(m + 1) * 128, t0:t0 + n], t_[:, 0:n], reads=[bt])
        S.barrier()
    with contextlib.ExitStack() as es:
        mg = [es.enter_context(self.sbt(f"mmg{i}", [128, NCH, 512], F32R)) for i in range(2)]
        b_mg = S.bufs(2, "mmg")
        wo = [es.enter_context(self.sbt(f"mwo{i}", [128, NCH, 128], F32R)) for i in range(NWB)]
        b_wo = S.bufs(NWB, "mwo")
        tm = [es.enter_context(self.sbt(f"mtn{i}", [128, 512], F32)) for i in range(2)]
        b_tm = S.bufs(2, "mtn")
        xs = [es.enter_context(self.sbt(f"mxs{i}", [128, 512], F32)) for i in range(2)]
        b_xs = S.bufs(2, "mxs")
        r = es.enter_context(self.sbt("mr", [128, NCH, 512], F32))
        b_r = S.bufs(NCH, "mr")
        lnb = _ln_bufs(self, es)
        wi = 0
        for ti, (t0, n, isctx) in enumerate(_tok_tiles(l)):
            g_, bg = mg[ti % 2], b_mg[ti % 2]
            S.dma("pool", g_[:, :, 0:n], self.mergedT[:, t0:t0 + n].rearrange("(k p) t -> p k t", p=128), writes=[bg])
            xsrc = self.csrc(l) if isctx else self.xsrc(l)
            xo = 0 if isctx else t0
            for m in range(NCH):
                w, bw = wo[wi % NWB], b_wo[wi % NWB]
                wi += 1
                x_, bx = xs[m % 2], b_xs[m % 2]
                t_, bt = tm[m % 2], b_tm[m % 2]
                S.dma("pool", w[:], self.w_out[l, :, m * 128:(m + 1) * 128].rearrange("(k p) n -> p k n", p=128), writes=[bw])
                S.dma("sp", x_[:, 0:n], xsrc[m * 128:(m + 1) * 128, xo:xo + n], writes=[bx])
                ps, pb = self.bank()
                for k in range(NCH):
                    S.op("pe", lambda e: e.matmul(ps[:, 0:n], w[:, k, :], g_[:, k, 0:n], start=(k == 0), stop=(k == NCH - 1)),
                         reads=[bw, bg], writes=[pb])
                S.op("act", lambda e: e.activation(out=t_[:, 0:n], in_=ps[:, 0:n], func=AF.Copy, scale=self.modcol(0, 2, m, isctx)),
                     reads=[pb, self.b_mod], writes=[bt])
                S.op("dve", lambda e: e.scalar_tensor_tensor(out=r[:, m, 0:n], in0=x_[:, 0:n], scalar=DN_ALPHA, in1=t_[:, 0:n], op0=ALU.mult, op1=ALU.add),
                     reads=[bx, bt], writes=[b_r[m]])
            x1dst = (lambda k: self.xc1T[k * 128:(k + 1) * 128, 0:n]) if isctx else (lambda k: self.x1T[k * 128:(k + 1) * 128, t0:t0 + n])
            hmdst = (lambda k: self.hmcT[k * 128:(k + 1) * 128, 0:n]) if isctx else (lambda k: self.hmT[k * 128:(k + 1) * 128, t0:t0 + n])
            _ln_tile(self, lnb, r, b_r, n, "ln1_g", "ln1_b",
                     [(x1dst, None, None), (hmdst, lambda k: self.op1(1, k, isctx), lambda k: self.modcol(0, 3, k, isctx))])
        S.barrier()


Prog.stage_ssmfin = _ssmfin_stage
Prog.stage_merge = _merge_stage


def _route_stage(self, l):
    nc, S = self.nc, self.S
    sets = [(self.hmT, TL, 256, self.hm_tok, 0)] + ([(self.hmcT, TC, 32, self.hmc_tok, 2)] if l == 0 else [])
    with contextlib.ExitStack() as es:
        rw = es.enter_context(self.sbt("rrw", [128, NCH, NE], F32))
        b_rw = S.buf("rrw")
        S.dma("sp", rw[:], self.router_w[l].rearrange("(k p) e -> p k e", p=128), writes=[b_rw])
        hm = [es.enter_context(self.sbt(f"rhm{i}", [128, NCH, 512], F32)) for i in range(2)]
        b_hm = S.bufs(2, "rhm")
        tk = [es.enter_context(self.sbt(f"rtk{i}", [128, D], F32)) for i in range(2)]
        b_tk = S.bufs(2, "rtk")
        sm = [es.enter_context(self.sbt(f"rsm{i}", [128, 64], F32)) for i in range(2)]
        b_sm = S.bufs(2, "rsm")
        affT = es.enter_context(self.sbt("raffT", [NE, TL], F32))
        b_aff = S.buf("raffT")
        topv = es.enter_context(self.sbt("rtopv", [NE, 256], F32))
        topi = es.enter_context(self.sbt("rtopi", [NE, 256], U32))
        topf = es.enter_context(self.sbt("rtopf", [NE, 256], F32))
        b_top = S.buf("rtop")
        zero = es.enter_context(self.sbt("rzero", [128, D], F32))
        b_zero = S.buf("rzero")
        S.op("pool", lambda e: e.memset(zero[:], 0.0), writes=[b_zero])
        for i in range(TL // 128):
            S.dma("sp", self.f_tok[i * 128:(i + 1) * 128, :], zero[:], reads=[b_zero])
        if l == 0:
            for i in range(3):
                S.dma("sp", self.fc_tok[i * 128:(i + 1) * 128, :], zero[:], reads=[b_zero])
                S.dma("sp", self.hmc_tok[i * 128:(i + 1) * 128, :], zero[:], reads=[b_zero])
        ident = self.cst("ident")
        if l == 0:
            S.op("dve", lambda e: e.tensor_copy(out=self.idxcol[:, 2, :], in_=self.cst("padidx")), reads=[self.b_ct], writes=[self.b_idx])
            S.op("dve", lambda e: e.memset(self.affcol[:, 2, :], 0.0), reads=[self.b_idx], writes=[self.b_idx])
        ti = ci = 0
        for (src, ntok, cap, tokdst, slot0) in sets:
            for t0 in range(0, ntok, 512):
                n = min(512, ntok - t0)
                h_, bh = hm[ti % 2], b_hm[ti % 2]
                ti += 1
                S.dma("sp", h_[:, :, 0:n], src[:, t0:t0 + n].rearrange("(k p) t -> p k t", p=128), writes=[bh])
                for c in range(n // 128):
                    tok = slice(c * 128, (c + 1) * 128)
                    t_, bt = tk[ci % 2], b_tk[ci % 2]
                    s_, bs = sm[ci % 2], b_sm[ci % 2]
                    ci += 1
                    for kg in range(4):
                        ps, pb = self.bank()
                        for kk in range(4):
                            k = kg * 4 + kk
                            S.op("pe", lambda e: e.transpose(ps[:, kk * 128:(kk + 1) * 128], h_[:, k, tok], ident), reads=[bh, self.b_ct], writes=[pb])
                        if kg % 2 == 0:
                            S.op("act", lambda e: e.activation(out=t_[:, kg * 512:(kg + 1) * 512], in_=ps[:, :], func=AF.Copy), reads=[pb], writes=[bt])
                        else:
                            S.op("dve", lambda e: e.tensor_copy(out=t_[:, kg * 512:(kg + 1) * 512], in_=ps[:, :]), reads=[pb], writes=[bt])
                    S.dma("sp", tokdst[t0 + c * 128:t0 + (c + 1) * 128, :], t_[:], reads=[bt])
                    psL, pbL = self.bank()
                    for k in range(NCH):
                        S.op("pe", lambda e: e.matmul(psL[:, 0:NE], h_[:, k, tok], rw[:, k, :], start=(k == 0), stop=(k == NCH - 1)),
                             reads=[bh, b_rw], writes=[pbL])
                    S.op("dve", lambda e: e.reduce_max(out=s_[:, 0:1], in_=psL[:, 0:NE], axis=AX.X), reads=[pbL], writes=[bs])
                    S.op("dve", lambda e: e.tensor_scalar(out=s_[:, 1:2], in0=s_[:, 0:1], scalar1=-1.0, scalar2=None, op0=ALU.mult), reads=[bs], writes=[bs])
                    S.op("act", lambda e: e.activation(out=s_[:, 16:32], in_=psL[:, 0:NE], func=AF.Exp, bias=s_[:, 1:2], scale=1.0, accum_out=s_[:, 2:3]),
                         reads=[pbL, bs], writes=[bs])
                    S.op("dve", lambda e: e.reciprocal(out=s_[:, 3:4], in_=s_[:, 2:3]), reads=[bs], writes=[bs])
                    S.op("dve", lambda e: e.tensor_scalar(out=s_[:, 32:48], in0=s_[:, 16:32], scalar1=s_[:, 3:4], scalar2=None, op0=ALU.mult), reads=[bs], writes=[bs])
                    psA, pbA = self.bank()
                    S.op("pe", lambda e: e.transpose(psA[0:NE, 0:128], s_[:, 32:48], ident), reads=[bs, self.b_ct], writes=[pbA])
                    S.op("dve", lambda e: e.tensor_copy(out=affT[:, t0 + c * 128:t0 + (c + 1) * 128], in_=psA[0:NE, 0:128]), reads=[pbA], writes=[b_aff])
            if "aff" in self.debug:
                S.dma("sp", self.dbg_aff[slot0 // 2][:, 0:ntok], affT[:, 0:ntok], reads=[b_aff])
            for r_ in range(cap // 8):
                sl = slice(8 * r_, 8 * r_ + 8)
                W = [b_aff, b_top]
                S.op("dve", lambda e: e.max(out=topv[:, sl], in_=affT[:, 0:ntok]), reads=W, writes=W)
                S.op("dve", lambda e: e.max_index(out=topi[:, sl], in_max=topv[:, sl], in_values=affT[:, 0:ntok]), reads=W, writes=W)
                S.op("dve", lambda e: e.match_replace(out=affT[:, 0:ntok], in_to_replace=topv[:, sl], in_values=affT[:, 0:ntok], imm_value=-1.0),
                     reads=W, writes=W)
            S.op("dve", lambda e: e.tensor_copy(out=topf[:, 0:cap], in_=topi[:, 0:cap]), reads=[b_top], writes=[b_top])
            for sc in range((cap + 127) // 128):
                ns = min(128, cap - sc * 128)
                ps, pb = self.bank()
                S.op("pe", lambda e: e.transpose(ps[0:ns, 0:NE], topf[:, sc * 128:sc * 128 + ns], ident[0:NE, 0:NE]), reads=[b_top, self.b_ct], writes=[pb])
                S.op("pe", lambda e: e.transpose(ps[0:ns, 32:32 + NE], topv[:, sc * 128:sc * 128 + ns], ident[0:NE, 0:NE]), reads=[b_top, self.b_ct], writes=[pb])
                S.op("dve", lambda e: e.tensor_copy(out=self.idxcol[0:ns, slot0 + sc, :], in_=ps[0:ns, 0:NE]), reads=[pb], writes=[self.b_idx])
                S.op("dve", lambda e: e.tensor_copy(out=self.affcol[0:ns, slot0 + sc, :], in_=ps[0:ns, 32:32 + NE]), reads=[pb], writes=[self.b_idx])
        if "aff" in self.debug:
            S.dma("sp", self.dbg_idx[:, :], self.idxcol[:].rearrange("p a b -> p (a b)"), reads=[self.b_idx])
            S.dma("sp", self.dbg_affcol[:, :], self.affcol[:].rearrange("p a b -> p (a b)"), reads=[self.b_idx])
        S.barrier()


def _expert_stage(self, l):
    nc, S = self.nc, self.S
    has_ctx = (l == 0)
    NS = 288 if has_ctx else 256
    chunks = [(0, 128, self.hm_tok, self.f_tok), (1, 128, self.hm_tok, self.f_tok)] + ([(2, 32, self.hmc_tok, self.fc_tok)] if has_ctx else [])
    with contextlib.ExitStack() as es:
        xs = [es.enter_context(self.sbt(f"exs{i}", [128, D], F32)) for i in range(2)]
        xs = [xs[0], xs[1], xs[0]]
        b_xs = S.bufs(2, "exs")
        b_xs = [b_xs[0], b_xs[1], b_xs[0]]
        xsT = es.enter_context(self.sbt("exsT", [128, NCH, 288], F32R))
        b_xsT = S.buf("exsT")
        wg = [es.enter_context(self.sbt(f"ewg{i}", [128, NCH, 256], F32R)) for i in range(3)]
        wu = [es.enter_context(self.sbt(f"ewu{i}", [128, NCH, 256], F32R)) for i in range(3)]
        b_wg, b_wu = S.bufs(3, "ewg"), S.bufs(3, "ewu")
        actT = es.enter_context(self.sbt("eact", [128, 8, 288], F32R))
        b_act = S.buf("eact")
        sg = [es.enter_context(self.sbt(f"esg{i}", [128, 288], F32)) for i in range(2)]
        b_sg = S.bufs(2, "esg")
        wd = [es.enter_context(self.sbt(f"ewd{i}", [128, 8, 512], F32R)) for i in range(2)]
        b_wd = S.bufs(2, "ewd")
        yt = [es.enter_context(self.sbt(f"eyt{i}", [128, D], F32)) for i in range(3)]
        b_yt = S.bufs(3, "eyt")
        ident = self.cst("ident")
        for i in range(3):
            S.op("pool", lambda e, i=i: e.memset(yt[i][:], 0.0), writes=[b_yt[i]])
        wi = si = di = 0
        b_ft = {id(self.f_tok): S.buf("f_tok"), id(self.fc_tok): S.buf("fc_tok")}
        for e_ in range(NE):
            for (sc, ns, tok, ftok) in chunks:
                S.idma(xs[sc][:, :], None, tok[:, :], bass.IndirectOffsetOnAxis(ap=self.idxcol[:, sc, e_:e_ + 1], axis=0),
                       reads=[self.b_idx], writes=[b_xs[sc]], owner=b_xs[sc])
                for kg in range(4):
                    ps, pb = self.bank()
                    for kk in range(4):
                        k = kg * 4 + kk
                        S.op("pe", lambda e: e.transpose(ps[:, kk * 128:kk * 128 + ns], xs[sc][0:ns, k * 128:(k + 1) * 128], ident[0:ns, 0:ns]),
                             reads=[b_xs[sc], self.b_ct], writes=[pb])
                    dst = xsT[:, kg * 4:(kg + 1) * 4, sc * 128:sc * 128 + ns]
                    srcv = ps[:, :].rearrange("p (a b) -> p a b", b=128)[:, :, 0:ns]
                    if kg % 2 == 0:
                        S.op("act", lambda e: e.activation(out=dst, in_=srcv, func=AF.Copy), reads=[pb], writes=[b_xsT])
                    else:
                        S.op("dve", lambda e: e.tensor_copy(out=dst, in_=srcv), reads=[pb], writes=[b_xsT])
            for fq in range(4):
                g_, bg = wg[wi % 3], b_wg[wi % 3]
                u_, bu = wu[wi % 3], b_wu[wi % 3]
                wi += 1
                S.dma("pool", g_[:], self.w_gate[l, e_, :, fq * 256:(fq + 1) * 256].rearrange("(k p) n -> p k n", p=128), writes=[bg])
                S.dma("pool", u_[:], self.w_up[l, e_, :, fq * 256:(fq + 1) * 256].rearrange("(k p) n -> p k n", p=128), writes=[bu])
                for fc in range(2):
                    psg, pbg = self.bank()
                    psu, pbu = self.bank()
                    for k in range(NCH):
                        S.op("pe", lambda e: e.matmul(psg[:, 0:NS], g_[:, k, fc * 128:(fc + 1) * 128], xsT[:, k, 0:NS], start=(k == 0), stop=(k == NCH - 1)),
                             reads=[bg, b_xsT], writes=[pbg])
                    for k in range(NCH):
                        S.op("pe", lambda e: e.matmul(psu[:, 0:NS], u_[:, k, fc * 128:(fc + 1) * 128], xsT[:, k, 0:NS], start=(k == 0), stop=(k == NCH - 1)),
                             reads=[bu, b_xsT], writes=[pbu])
                    s_, bs = sg[si % 2], b_sg[si % 2]
                    si += 1
                    S.op("act", lambda e: e.activation(out=s_[:, 0:NS], in_=psg[:, 0:NS], func=AF.Silu), reads=[pbg], writes=[bs])
                    S.op("dve", lambda e: e.tensor_tensor(out=actT[:, fq * 2 + fc, 0:NS], in0=psu[:, 0:NS], in1=s_[:, 0:NS], op=ALU.mult),
                         reads=[pbu, bs], writes=[b_act])
            for dt in range(4):
                d_, bd = wd[di % 2], b_wd[di % 2]
                di += 1
                S.dma("pool", d_[:], self.w_down[l, e_, :, dt * 512:(dt + 1) * 512].rearrange("(k p) n -> p k n", p=128), writes=[bd])
                for (sc, ns, tok, ftok) in chunks:
                    ps, pb = self.bank()
                    for fk in range(8):
                        S.op("pe", lambda e: e.matmul(ps[0:ns, :], actT[:, fk, sc * 128:sc * 128 + ns], d_[:, fk, :], start=(fk == 0), stop=(fk == 7)),
                             reads=[b_act, bd], writes=[pb])
                    S.op("act", lambda e: e.activation(out=yt[sc][0:ns, dt * 512:(dt + 1) * 512], in_=ps[0:ns, :], func=AF.Copy,
                                                       scale=self.affcol[0:ns, sc, e_:e_ + 1]),
                         reads=[pb, self.b_idx], writes=[b_yt[sc]])
            for (sc, ns, tok, ftok) in chunks:
                bf = b_ft[id(ftok)]
                S.idma(ftok[:, :], bass.IndirectOffsetOnAxis(ap=self.idxcol[:, sc, e_:e_ + 1], axis=0), yt[sc][:, :], None,
                       reads=[b_yt[sc], self.b_idx], writes=[bf], owner=b_yt[sc], compute_op=ALU.add)
        S.barrier()


def _ln2_stage(self, l):
    nc, S = self.nc, self.S
    last = (l == DEPTH - 1)
    with contextlib.ExitStack() as es:
        ft = [es.enter_context(self.sbt(f"nft{i}", [128, 4, D], F32)) for i in range(1)]
        b_ft = S.bufs(1, "nft")
        xs = [es.enter_context(self.sbt(f"nxs{i}", [128, 512], F32)) for i in range(2)]
        b_xs = S.bufs(2, "nxs")
        tm = [es.enter_context(self.sbt(f"ntm{i}", [128, 512], F32)) for i in range(2)]
        b_tm = S.bufs(2, "ntm")
        r = es.enter_context(self.sbt("nr", [128, NCH, 512], F32))
        b_r = S.bufs(NCH, "nr")
        lnb = _ln_bufs(self, es)
        ident = self.cst("ident")
        for (t0, n, isctx) in _tok_tiles(l):
            fsrc = self.fc_tok if isctx else self.f_tok
            xsrc = self.xc1T if isctx else self.x1T
            fo = 0 if isctx else t0
            f_, bf = ft[0], b_ft[0]
            S.dma("sp", f_[:, 0:n // 128, :], fsrc[fo:fo + n, :].rearrange("(c p) d -> p c d", p=128), writes=[bf])
            for k in range(NCH):
                ps, pb = self.bank()
                for c in range(n // 128):
                    S.op("pe", lambda e: e.transpose(ps[:, c * 128:(c + 1) * 128], f_[:, c, k * 128:(k + 1) * 128], ident), reads=[bf, self.b_ct], writes=[pb])
                x_, bx = xs[k % 2], b_xs[k % 2]
                t_, bt = tm[k % 2], b_tm[k % 2]
                S.dma("sp", x_[:, 0:n], xsrc[k * 128:(k + 1) * 128, fo:fo + n], writes=[bx])
                S.op("act", lambda e: e.activation(out=t_[:, 0:n], in_=ps[:, 0:n], func=AF.Copy, scale=self.modcol(0, 5, k, isctx)),
                     reads=[pb, self.b_mod], writes=[bt])
                S.op("dve", lambda e: e.scalar_tensor_tensor(out=r[:, k, 0:n], in0=x_[:, 0:n], scalar=DN_ALPHA, in1=t_[:, 0:n], op0=ALU.mult, op1=ALU.add),
                     reads=[bx, bt], writes=[b_r[k]])
            if isctx:
                dst = lambda k: self.xc2T[k * 128:(k + 1) * 128, 0:n]
            elif last:
                dst = lambda k: self.outT[k * 128:(k + 1) * 128, t0:t0 + n]
            else:
                dst = lambda k: self.x2T[k * 128:(k + 1) * 128, t0:t0 + n]
            _ln_tile(self, lnb, r, b_r, n, "ln2_g", "ln2_b", [(dst, None, None)])
        S.barrier()


Prog.stage_route = _route_stage
Prog.stage_expert = _expert_stage
Prog.stage_ln2 = _ln2_stage


_CACHE = {}


def _program():
    if "nc" not in _CACHE:
        P = Prog(layers=(0, 1))
        _CACHE["nc"] = P.build()
    return _CACHE["nc"]


def kernel(**inputs):
    inp = {k: np.asarray(v) for k, v in inputs.items()}
    nc = _program()
    consts = make_consts()
    cols = make_cols(inp)
    n = 8
    shared = core_inputs(inp, 0, consts, cols)
    in_maps = []
    for b in range(n):
        m = dict(shared)
        cc = np.stack([fm(inp["c"][b], 16), fm(inp["c_ctx"], 16)], axis=-1).astype(np.float32)
        m["xT"] = np.ascontiguousarray(inp["x"][b].T)
        m["ctxT"] = np.ascontiguousarray(inp["ctx"][b].T)
        m["ccol"] = np.ascontiguousarray(cc)
        in_maps.append(m)
    res = run_bass_kernel_spmd(nc, in_maps, core_ids=list(range(n)))
    out = np.stack([np.ascontiguousarray(res.results[b]["outT"].T) for b in range(n)], axis=0)
    return out.astype(np.float32)


def _ssm3_stage(self, l):
    nc, S = self.nc, self.S
    T = self.T
    TWO_PI = 2.0 * math.pi
    with contextlib.ExitStack() as es:
        sp = es.enter_context(self.sbt("sp_", [128, 2, SSMP_W], F32))
        b_sp = S.buf("sp")
        S.dma("sp", sp[:], self.ssmp[l].rearrange("d p w -> p d w"), writes=[b_sp])
        pr = es.enter_context(self.sbt("spr", [128, 16, 64], F32))
        pi_ = es.enter_context(self.sbt("spi", [128, 2, 64], I32))
        b_pr = S.buf("spr")
        bb = es.enter_context(self.sbt("sbb", [128, 2, 2, 32, 16], F32))
        b_bb = S.buf("sbb")
        tmpb = es.enter_context(self.sbt("stmpb", [128, 32, 16], F32))

        def P_(i):
            return pr[:, i, :]
        LAMR, LAMI, DT, MAG, TH, SINT, COST, ARE, AIM, CORE, COIM, T0, T1, T2, NSIN, DEN = range(16)
        for d in range(2):
            S.op("dve", lambda e, d=d: e.tensor_scalar(out=pr[:, LAMR, d * 32:(d + 1) * 32], in0=sp[:, d, 0:32], scalar1=-1e-4, scalar2=None, op0=ALU.min),
                 reads=[b_sp], writes=[b_pr])
            S.op("dve", lambda e, d=d: e.tensor_copy(out=pr[:, LAMI, d * 32:(d + 1) * 32], in_=sp[:, d, 32:64]), reads=[b_sp], writes=[b_pr])
            S.op("act", lambda e, d=d: e.activation(out=pr[:, DT, d * 32:(d + 1) * 32], in_=sp[:, d, 64:96], func=AF.Exp), reads=[b_sp], writes=[b_pr])
        W = [b_pr]

        def dve(fn):
            S.op("dve", fn, reads=W, writes=W)

        def act(fn):
            S.op("act", fn, reads=W, writes=W)
        dve(lambda e: e.tensor_tensor(out=P_(T0), in0=P_(LAMR), in1=P_(DT), op=ALU.mult))
        act(lambda e: e.activation(out=P_(MAG), in_=P_(T0), func=AF.Exp))
        dve(lambda e: e.tensor_tensor(out=P_(TH), in0=P_(LAMI), in1=P_(DT), op=ALU.mult))

        def sin_of(dst, shift):
            dve(lambda e: e.tensor_scalar(out=P_(T0), in0=P_(TH), scalar1=shift, scalar2=None, op0=ALU.add))
            dve(lambda e: e.tensor_scalar(out=P_(T1), in0=P_(T0), scalar1=1.0 / TWO_PI, scalar2=64.5, op0=ALU.mult, op1=ALU.add))
            dve(lambda e: e.tensor_copy(out=pi_[:, 0, :], in_=P_(T1)))
            dve(lambda e: e.tensor_copy(out=P_(T1), in_=pi_[:, 0, :]))
            dve(lambda e: e.tensor_scalar(out=P_(T1), in0=P_(T1), scalar1=-64.0, scalar2=-TWO_PI, op0=ALU.add, op1=ALU.mult))
            dve(lambda e: e.tensor_tensor(out=P_(T0), in0=P_(T0), in1=P_(T1), op=ALU.add))
            dve(lambda e: e.tensor_scalar(out=P_(T1), in0=P_(T0), scalar1=-math.pi, scalar2=TWO_PI, op0=ALU.is_lt, op1=ALU.mult))
            dve(lambda e: e.tensor_tensor(out=P_(T0), in0=P_(T0), in1=P_(T1), op=ALU.add))
            dve(lambda e: e.tensor_scalar(out=P_(T1), in0=P_(T0), scalar1=math.pi, scalar2=-TWO_PI, op0=ALU.is_gt, op1=ALU.mult))
            dve(lambda e: e.tensor_tensor(out=P_(T0), in0=P_(T0), in1=P_(T1), op=ALU.add))
            dve(lambda e: e.tensor_scalar(out=P_(T0), in0=P_(T0), scalar1=math.pi, scalar2=-math.pi, op0=ALU.min, op1=ALU.max))
            act(lambda e: e.activation(out=P_(dst), in_=P_(T0), func=AF.Sin))
        sin_of(SINT, 0.0)
        sin_of(COST, math.pi / 2)
        dve(lambda e: e.tensor_tensor(out=P_(ARE), in0=P_(MAG), in1=P_(COST), op=ALU.mult))
        dve(lambda e: e.tensor_tensor(out=P_(AIM), in0=P_(MAG), in1=P_(SINT), op=ALU.mult))
        dve(lambda e: e.tensor_scalar(out=P_(NSIN), in0=P_(SINT), scalar1=-1.0, scalar2=None, op0=ALU.mult))
        dve(lambda e: e.tensor_tensor(out=P_(T0), in0=P_(LAMR), in1=P_(LAMR), op=ALU.mult))
        dve(lambda e: e.tensor_tensor(out=P_(T1), in0=P_(LAMI), in1=P_(LAMI), op=ALU.mult))
        dve(lambda e: e.tensor_tensor(out=P_(DEN), in0=P_(T0), in1=P_(T1), op=ALU.add))
        dve(lambda e: e.reciprocal(out=P_(DEN), in_=P_(DEN)))
        dve(lambda e: e.tensor_scalar(out=P_(T2), in0=P_(ARE), scalar1=-1.0, scalar2=None, op0=ALU.add))
        dve(lambda e: e.tensor_tensor(out=P_(T0), in0=P_(T2), in1=P_(LAMR), op=ALU.mult))
        dve(lambda e: e.tensor_tensor(out=P_(T1), in0=P_(AIM), in1=P_(LAMI), op=ALU.mult))
        dve(lambda e: e.tensor_tensor(out=P_(T0), in0=P_(T0), in1=P_(T1), op=ALU.add))
        dve(lambda e: e.tensor_tensor(out=P_(CORE), in0=P_(T0), in1=P_(DEN), op=ALU.mult))
        dve(lambda e: e.tensor_tensor(out=P_(T0), in0=P_(AIM), in1=P_(LAMR), op=ALU.mult))
        dve(lambda e: e.tensor_tensor(out=P_(T1), in0=P_(T2), in1=P_(LAMI), op=ALU.mult))
        dve(lambda e: e.tensor_tensor(out=P_(T0), in0=P_(T0), in1=P_(T1), op=ALU.subtract))
        dve(lambda e: e.tensor_tensor(out=P_(COIM), in0=P_(T0), in1=P_(DEN), op=ALU.mult))
        W2 = [b_pr, b_sp, b_bb]
        for d in range(2):
            bre = sp[:, d, 96:96 + 512].rearrange("p (c h) -> p c h", h=16)
            bim = sp[:, d, 96 + 512:96 + 1024].rearrange("p (c h) -> p c h", h=16)
            cor = pr[:, CORE, d * 32:(d + 1) * 32].unsqueeze(2).to_broadcast([128, 32, 16])
            coi = pr[:, COIM, d * 32:(d + 1) * 32].unsqueeze(2).to_broadcast([128, 32, 16])
            S.op("dve", lambda e: e.tensor_tensor(out=bb[:, d, 0], in0=bre, in1=cor, op=ALU.mult), reads=W2, writes=W2)
            S.op("dve", lambda e: e.tensor_tensor(out=tmpb[:], in0=bim, in1=coi, op=ALU.mult), reads=W2, writes=W2)
            S.op("dve", lambda e: e.tensor_tensor(out=bb[:, d, 0], in0=bb[:, d, 0], in1=tmpb[:], op=ALU.subtract), reads=W2, writes=W2)
            S.op("dve", lambda e: e.tensor_tensor(out=bb[:, d, 1], in0=bim, in1=cor, op=ALU.mult), reads=W2, writes=W2)
            S.op("dve", lambda e: e.tensor_tensor(out=tmpb[:], in0=bre, in1=coi, op=ALU.mult), reads=W2, writes=W2)
            S.op("dve", lambda e: e.tensor_tensor(out=bb[:, d, 1], in0=bb[:, d, 1], in1=tmpb[:], op=ALU.add), reads=W2, writes=W2)
        if "ssm_dbg" in self.debug:
            dbg = self.dout("ssm_dbg", [128, 16 * 64 + 2 * 2 * 512])
            S.dma("sp", dbg[:, 0:1024], pr[:].rearrange("p a b -> p (a b)"), reads=W2)
            S.dma("sp", dbg[:, 1024:], bb[:].rearrange("p a b c h -> p (a b c h)"), reads=W2)

        CH = 8
        apr = es.enter_context(self.sbt("sapr", [128, CH + 1, 64], F32))
        api = es.enter_context(self.sbt("sapi", [128, CH + 1, 64], F32))
        pt = [es.enter_context(self.sbt(f"spt{i}", [128, 8, 64], F32)) for i in range(2)]
        b_ap = S.buf("sap")
        WA = [b_pr, b_ap]
        S.op("dve", lambda e: e.memset(apr[:, 0, :], 1.0), reads=WA, writes=WA)
        S.op("dve", lambda e: e.memset(api[:, 0, :], 0.0), reads=WA, writes=WA)
        S.op("dve", lambda e: e.tensor_copy(out=apr[:, 1, :], in_=P_(ARE)), reads=WA, writes=WA)
        S.op("dve", lambda e: e.tensor_copy(out=api[:, 1, :], in_=P_(AIM)), reads=WA, writes=WA)
        m = 1
        while m < CH:
            sr = apr[:, m:m + 1, :].to_broadcast([128, m, 64])
            si = api[:, m:m + 1, :].to_broadcast([128, m, 64])
            xr, xi = apr[:, 1:1 + m, :], api[:, 1:1 + m, :]
            dr, di = apr[:, 1 + m:1 + 2 * m, :], api[:, 1 + m:1 + 2 * m, :]
            t0_, t1_ = pt[0][:, 0:m, :], pt[1][:, 0:m, :]
            S.op("dve", lambda e: e.tensor_tensor(out=t0_, in0=xr, in1=sr, op=ALU.mult), reads=WA, writes=WA)
            S.op("dve", lambda e: e.tensor_tensor(out=t1_, in0=xi, in1=si, op=ALU.mult), reads=WA, writes=WA)
            S.op("dve", lambda e: e.tensor_tensor(out=dr, in0=t0_, in1=t1_, op=ALU.subtract), reads=WA, writes=WA)
            S.op("dve", lambda e: e.tensor_tensor(out=t0_, in0=xr, in1=si, op=ALU.mult), reads=WA, writes=WA)
            S.op("dve", lambda e: e.tensor_tensor(out=t1_, in0=xi, in1=sr, op=ALU.mult), reads=WA, writes=WA)
            S.op("dve", lambda e: e.tensor_tensor(out=di, in0=t0_, in1=t1_, op=ALU.add), reads=WA, writes=WA)
            m *= 2
        R16, C16, NS16 = T0, T1, T2
        S.op("dve", lambda e: e.tensor_tensor(out=P_(R16), in0=P_(MAG), in1=P_(MAG), op=ALU.mult), reads=WA, writes=WA)
        for _ in range(int(math.log2(CH)) - 1):
            S.op("dve", lambda e: e.tensor_tensor(out=P_(R16), in0=P_(R16), in1=P_(R16), op=ALU.mult), reads=WA, writes=WA)
        S.op("dve", lambda e: e.reciprocal(out=P_(DEN), in_=P_(R16)), reads=WA, writes=WA)
        S.op("dve", lambda e: e.tensor_tensor(out=P_(C16), in0=apr[:, CH, :], in1=P_(DEN), op=ALU.mult), reads=WA, writes=WA)
        S.op("dve", lambda e: e.scalar_tensor_tensor(out=P_(NS16), in0=api[:, CH, :], scalar=-1.0, in1=P_(DEN), op0=ALU.mult, op1=ALU.mult),
             reads=WA, writes=WA)

        NCK = T // CH
        uu = [es.enter_context(self.sbt(f"suu{i}", [128, CH, NCK], F32R)) for i in range(1)] * 2
        b_uu = S.bufs(1, "suu") * 2
        yq = [es.enter_context(self.sbt(f"syq{i}", [128, T], F32)) for i in range(1)] * 2
        b_yq = S.bufs(1, "syq") * 2
        wsm2 = [es.enter_context(self.sbt(f"swsm{i}", [128, 2, CH, 128], F32)) for i in range(2)]
        b_wsm2 = S.bufs(2, "swsm")
        winT2 = [es.enter_context(self.sbt(f"swinT{i}", [128, 2, CH, 128], F32R)) for i in range(2)]
        b_winT2 = S.bufs(2, "swinT")
        Oq = [es.enter_context(self.sbt(f"sO{j}", [128, 2, CH + 1, 128], F32R)) for j in range(4)]
        b_O = S.bufs(4, "sO")
        bbk = [es.enter_context(self.sbt(f"sbbk{j}", [128, 2, 128], F32R)) for j in range(4)]
        b_bbk = S.bufs(4, "sbbk")
        KT = es.enter_context(self.sbt("sKT", [128, CH, 128], F32R))
        b_KT = S.buf("sKT")
        Sp = [es.enter_context(self.sbt(f"sSp{j}", [128, 2, NCK], F32R)) for j in range(4)]
        b_Sp = S.bufs(4, "sSp")
        Rt2 = [[es.enter_context(self.sbt(f"sR{c}{i}", [128, NCK + 1], F32)) for i in range(2)] for c in range(2)]
        tq2 = [[es.enter_context(self.sbt(f"stq{c}{i}", [128, (CH + 1) * 16], F32)) for i in range(4)] for c in range(2)]
        sv2 = [[es.enter_context(self.sbt(f"ssv{c}{i}", [128, NCK], F32)) for i in range(6)] for c in range(2)]
        b_w2 = S.bufs(2, "swork")
        for c in range(2):
            S.op("dve", lambda e, c=c: e.memset(wsm2[c][:], 0.0), writes=[b_wsm2[c]])
        for j in range(4):
            S.op("pool", lambda e, j=j: e.memset(Oq[j][:].bitcast(F32), 0.0), writes=[b_O[j]])
            S.op("pool", lambda e, j=j: e.memset(bbk[j][:].bitcast(F32), 0.0), writes=[b_bbk[j]])
            S.op("pool", lambda e, j=j: e.memset(Sp[j][:].bitcast(F32), 0.0), writes=[b_Sp[j]])
        ident = self.cst("ident")
        b_ys = S.bufs(8, "ysT")
        eic = [0]
        for d in range(2):
            for q in range(8):
                u_ = uu[q % 2]
                bu = b_uu[q % 2]
                udst = u_[:, :, :].rearrange("p j c -> p c j")
                if d == 0:
                    S.dma("sp", yq[0][:, 0:TC], self.uT[q * 128:(q + 1) * 128, TL:T], writes=[b_yq[0]])
                    S.dma("sp", yq[0][:, TC:T], self.uT[q * 128:(q + 1) * 128, 0:TL], writes=[b_yq[0]])
                    S.op("act", lambda e: e.activation(out=udst, in_=yq[0][:, :].rearrange("p (c j) -> p c j", j=CH), func=AF.Copy),
                         reads=[b_yq[0]], writes=[bu])
                else:
                    S.dma("sp", yq[0][:, :], self.uT[q * 128:(q + 1) * 128, :], writes=[b_yq[0]])
                    S.op("act", lambda e: e.activation(out=udst, in_=yq[0][:, ::-1].rearrange("p (c j) -> p c j", j=CH), func=AF.Copy),
                         reads=[b_yq[0]], writes=[bu])
                def sc_work(j, ch):
                    wsm, b_wsm, winT, b_winT = wsm2[ch], b_wsm2[ch], winT2[ch], b_winT2[ch]
                    Rt, tq, sv, b_w = Rt2[ch], tq2[ch], sv2[ch], b_w2[ch]
                    sc = 4 * q + j
                    cidx = d * 32 + sc
                    c0 = 32 * j
                    O_, bO = Oq[j], b_O[j]
                    k_, bk = bbk[j], b_bbk[j]
                    WW = [b_w, b_bb, b_sp, b_ap]
                    for half in range(2):
                        rows = slice(64 * half, 64 * half + 64)
                        cols = slice(c0 + 16 * half, c0 + 16 * half + 16)
                        bbr = bb[rows, d, 0, sc, :].unsqueeze(1).to_broadcast([64, CH, 16])
                        bbi = bb[rows, d, 1, sc, :].unsqueeze(1).to_broadcast([64, CH, 16])
                        ar16 = apr[rows, 0:CH, cidx:cidx + 1].to_broadcast([64, CH, 16])
                        ai16 = api[rows, 0:CH, cidx:cidx + 1].to_broadcast([64, CH, 16])
                        t = [tq[i][rows, 0:CH * 16].rearrange("p (k h) -> p k h", h=16) for i in range(4)]
                        S.op("dve", lambda e: e.tensor_tensor(out=t[0], in0=bbr, in1=ar16, op=ALU.mult), reads=WW, writes=[b_w])
                        yield
                        S.op("dve", lambda e: e.tensor_tensor(out=t[1], in0=bbi, in1=ai16, op=ALU.mult), reads=WW, writes=[b_w])
                        yield
                        S.op("dve", lambda e: e.tensor_tensor(out=wsm[rows, 0, :, cols], in0=t[0], in1=t[1], op=ALU.subtract), reads=[b_w, b_wsm], writes=[b_wsm])
                        yield
                        S.op("dve", lambda e: e.tensor_tensor(out=t[2], in0=bbr, in1=ai16, op=ALU.mult), reads=WW, writes=[b_w])
                        yield
                        S.op("dve", lambda e: e.tensor_tensor(out=t[3], in0=bbi, in1=ar16, op=ALU.mult), reads=WW, writes=[b_w])
                        yield
                        S.op("dve", lambda e: e.tensor_tensor(out=wsm[rows, 1, :, cols], in0=t[2], in1=t[3], op=ALU.add), reads=[b_w, b_wsm], writes=[b_wsm])
                        yield
                        cre = sp[rows, d, 96 + 1024 + sc * 16:96 + 1024 + sc * 16 + 16].unsqueeze(1).to_broadcast([64, CH + 1, 16])
                        cim = sp[rows, d, 96 + 1536 + sc * 16:96 + 1536 + sc * 16 + 16].unsqueeze(1).to_broadcast([64, CH + 1, 16])
                        ar17 = apr[rows, 0:CH + 1, cidx:cidx + 1].to_broadcast([64, CH + 1, 16])
                        ai17 = api[rows, 0:CH + 1, cidx:cidx + 1].to_broadcast([64, CH + 1, 16])
                        t = [tq[i][rows, 0:(CH + 1) * 16].rearrange("p (k h) -> p k h", h=16) for i in range(4)]
                        S.op("dve", lambda e: e.tensor_tensor(out=t[0], in0=cre, in1=ar17, op=ALU.mult), reads=WW, writes=[b_w])
                        yield
                        S.op("dve", lambda e: e.tensor_tensor(out=t[1], in0=cim, in1=ai17, op=ALU.mult), reads=WW, writes=[b_w])
                        yield
                        S.op("dve", lambda e: e.tensor_tensor(out=O_[rows, 0, :, cols], in0=t[0], in1=t[1], op=ALU.subtract), reads=[b_w, bO], writes=[bO])
                        yield
                        S.op("dve", lambda e: e.scalar_tensor_tensor(out=t[2], in0=cre, scalar=-1.0, in1=ai17, op0=ALU.mult, op1=ALU.mult), reads=WW, writes=[b_w])
                        yield
                        S.op("dve", lambda e: e.scalar_tensor_tensor(out=t[3], in0=cim, scalar=-1.0, in1=ar17, op0=ALU.mult, op1=ALU.mult), reads=WW, writes=[b_w])
                        yield
                        S.op("dve", lambda e: e.tensor_tensor(out=O_[rows, 1, :, cols], in0=t[2], in1=t[3], op=ALU.add), reads=[b_w, bO], writes=[bO])
                        yield
                        for ri in range(2):
                            S.op("act", lambda e: e.activation(out=k_[rows, ri, cols], in_=bb[rows, d, ri, sc, :], func=AF.Copy), reads=[b_bb, bk], writes=[bk])
                            yield
                    for ri in range(2):
                        for kg in range(CH // 4):
                            ps, pb = self.bank(ch)
                            for kk in range(4):
                                S.op("pe", lambda e: e.transpose(ps[:, kk * 128:(kk + 1) * 128], wsm[:, ri, kg * 4 + kk, :], ident), reads=[b_wsm, self.b_ct], writes=[pb])
                                yield
                            dst = winT[:, ri, kg * 4:(kg + 1) * 4, :]
                            srcv = ps[:, :].rearrange("p (a b) -> p a b", b=128)
                            eic[0] += 1
                            if eic[0] % 2 == 0:
                                S.op("act", lambda e: e.activation(out=dst, in_=srcv, func=AF.Copy), reads=[pb], writes=[b_winT])
                                yield
                            else:
                                S.op("dve", lambda e: e.tensor_copy(out=dst, in_=srcv), reads=[pb], writes=[b_winT])
                                yield
                    S.op("pool", lambda e: e.memset(wsm[:, :, :, c0:c0 + 32], 0.0), reads=[b_wsm], writes=[b_wsm])
                    yield
                    psV = []
                    for ri in range(2):
                        ps, pb = self.bank(ch)
                        psV.append((ps, pb))
                        for jj in range(CH):
                            S.op("pe", lambda e: e.matmul(ps[:, 0:NCK], winT[:, ri, CH - 1 - jj, :], u_[:, jj, :], start=(jj == 0), stop=(jj == CH - 1)),
                                 reads=[b_winT, bu], writes=[pb])
                            yield
                    RW = [b_w, b_pr]
                    S.op("dve", lambda e: e.memset(Rt[0][:, 0:1], 1.0), reads=RW, writes=[b_w])
                    yield
                    S.op("dve", lambda e: e.memset(Rt[1][:, 0:1], 0.0), reads=RW, writes=[b_w])
                    yield
                    S.op("dve", lambda e: e.tensor_copy(out=Rt[0][:, 1:2], in_=pr[:, C16, cidx:cidx + 1]), reads=RW, writes=[b_w])
                    yield
                    S.op("dve", lambda e: e.tensor_copy(out=Rt[1][:, 1:2], in_=pr[:, NS16, cidx:cidx + 1]), reads=RW, writes=[b_w])
                    yield
                    m = 1
                    while m < NCK:
                        n_ = min(m, NCK - m)
                        sr, si = Rt[0][:, m:m + 1], Rt[1][:, m:m + 1]
                        src_r, src_i = Rt[0][:, 1:1 + n_], Rt[1][:, 1:1 + n_]
                        dst_r, dst_i = Rt[0][:, 1 + m:1 + m + n_], Rt[1][:, 1 + m:1 + m + n_]
                        tmp = sv[0][:, 0:n_]
                        S.op("dve", lambda e: e.tensor_scalar(out=tmp, in0=src_i, scalar1=si, scalar2=None, op0=ALU.mult), reads=RW, writes=[b_w])
                        yield
                        S.op("dve", lambda e: e.scalar_tensor_tensor(out=dst_r, in0=src_r, scalar=sr, in1=tmp, op0=ALU.mult, op1=ALU.subtract), reads=RW, writes=[b_w])
                        yield
                        S.op("dve", lambda e: e.tensor_scalar(out=tmp, in0=src_i, scalar1=sr, scalar2=None, op0=ALU.mult), reads=RW, writes=[b_w])
                        yield
                        S.op("dve", lambda e: e.scalar_tensor_tensor(out=dst_i, in0=src_r, scalar=si, in1=tmp, op0=ALU.mult, op1=ALU.add), reads=RW, writes=[b_w])
                        yield
                        m += n_
                    Rr, Ri = Rt[0][:, 0:NCK], Rt[1][:, 0:NCK]
                    (pvr, pbr), (pvi, pbi) = psV
                    A = [pbr, pbi, b_w]
                    S.op("dve", lambda e: e.tensor_tensor(out=sv[0][:], in0=pvr[:, 0:NCK], in1=Rr, op=ALU.mult), reads=A, writes=[b_w])
                    yield
                    S.op("dve", lambda e: e.tensor_tensor(out=sv[1][:], in0=pvi[:, 0:NCK], in1=Ri, op=ALU.mult), reads=A, writes=[b_w])
                    yield
                    S.op("dve", lambda e: e.tensor_tensor(out=sv[2][:], in0=pvr[:, 0:NCK], in1=Ri, op=ALU.mult), reads=A, writes=[b_w])
                    yield
                    S.op("dve", lambda e: e.tensor_tensor(out=sv[3][:], in0=pvi[:, 0:NCK], in1=Rr, op=ALU.mult), reads=A, writes=[b_w])
                    yield
                    S.op("dve", lambda e: e.tensor_tensor(out=sv[0][:], in0=sv[0][:], in1=sv[1][:], op=ALU.subtract), reads=[b_w], writes=[b_w])
                    yield
                    S.op("dve", lambda e: e.tensor_tensor(out=sv[2][:], in0=sv[2][:], in1=sv[3][:], op=ALU.add), reads=[b_w], writes=[b_w])
                    yield
                    dec = pr[:, R16, cidx:cidx + 1].to_broadcast([128, NCK])
                    S.op("dve", lambda e: e.tensor_tensor_scan(sv[4][:], dec, sv[0][:], 0.0, ALU.mult, ALU.add), reads=[b_w, b_pr], writes=[b_w])
                    yield
                    S.op("dve", lambda e: e.tensor_tensor_scan(sv[5][:], dec, sv[2][:], 0.0, ALU.mult, ALU.add), reads=[b_w, b_pr], writes=[b_w])
                    yield
                    n1 = NCK - 1
                    S.op("dve", lambda e: e.tensor_tensor(out=sv[0][:, 0:n1], in0=sv[4][:, 0:n1], in1=Rr[:, 0:n1], op=ALU.mult), reads=[b_w], writes=[b_w])
                    yield
                    S.op("dve", lambda e: e.tensor_tensor(out=sv[1][:, 0:n1], in0=sv[5][:, 0:n1], in1=Ri[:, 0:n1], op=ALU.mult), reads=[b_w], writes=[b_w])
                    yield
                    S.op("dve", lambda e: e.tensor_tensor(out=sv[2][:, 0:n1], in0=sv[5][:, 0:n1], in1=Rr[:, 0:n1], op=ALU.mult), reads=[b_w], writes=[b_w])
                    yield
                    S.op("dve", lambda e: e.tensor_tensor(out=sv[3][:, 0:n1], in0=sv[4][:, 0:n1], in1=Ri[:, 0:n1], op=ALU.mult), reads=[b_w], writes=[b_w])
                    yield
                    S.op("dve", lambda e: e.tensor_tensor(out=Sp[j][:, 0, 1:NCK], in0=sv[0][:, 0:n1], in1=sv[1][:, 0:n1], op=ALU.add), reads=[b_w, b_Sp[j]], writes=[b_Sp[j]])
                    yield
                    S.op("dve", lambda e: e.tensor_tensor(out=Sp[j][:, 1, 1:NCK], in0=sv[2][:, 0:n1], in1=sv[3][:, 0:n1], op=ALU.subtract), reads=[b_w, b_Sp[j]], writes=[b_Sp[j]])
                    yield
                for pair in ((0, 1), (2, 3)):
                    gens = [sc_work(j, j % 2) for j in pair]
                    while gens:
                        for g_ in list(gens):
                            try:
                                next(g_)
                            except StopIteration:
                                gens.remove(g_)
                for kb in range(CH // 4):
                    ps, pb = self.bank()
                    n_mm = 0
                    for j in range(4):
                        for ri in range(2):
                            S.op("pe", lambda e: e.matmul(ps[:, :], bbk[j][:, ri, :], Oq[j][:, ri, kb * 4:(kb + 1) * 4, :], start=(n_mm == 0), stop=(n_mm == 7)),
                                 reads=[b_bbk[j], b_O[j]], writes=[pb])
                            n_mm += 1
                    dst = KT[:, kb * 4:(kb + 1) * 4, :]
                    srcv = ps[:, :].rearrange("p (a b) -> p a b", b=128)
                    if kb % 2 == 0:
                        S.op("act", lambda e: e.activation(out=dst, in_=srcv, func=AF.Copy), reads=[pb], writes=[b_KT])
                    else:
                        S.op("dve", lambda e: e.tensor_copy(out=dst, in_=srcv), reads=[pb], writes=[b_KT])
                if "s3_dbg" in self.debug and q == 0:
                    dbg = self.dram.get("s3_dbg") or self.dout("s3_dbg", [2, 128, 2048 + 4 * 2 * NCK])
                    S.dma("sp", dbg[d, :, 0:CH * 128], KT[:].bitcast(F32).rearrange("p a b -> p (a b)"), reads=[b_KT])
                    for j in range(4):
                        S.dma("sp", dbg[d, :, 2048 + j * 2 * NCK:2048 + (j + 1) * 2 * NCK], Sp[j][:].bitcast(F32).rearrange("p a b -> p (a b)"), reads=[b_Sp[j]])
                y_, by = yq[q % 2], b_yq[q % 2]
                for tau in range(CH):
                    ps, pb = self.bank()
                    nmm = (tau + 1) + 8
                    i_mm = 0
                    for jj in range(tau + 1):
                        S.op("pe", lambda e: e.matmul(ps[:, 0:NCK], KT[:, tau - jj, :], u_[:, jj, :], start=(i_mm == 0), stop=(i_mm == nmm - 1)),
                             reads=[b_KT, bu], writes=[pb])
                        i_mm += 1
                    for j in range(4):
                        for ri in range(2):
                            S.op("pe", lambda e: e.matmul(ps[:, 0:NCK], Oq[j][:, ri, tau + 1, :], Sp[j][:, ri, :], start=(i_mm == 0), stop=(i_mm == nmm - 1)),
                                 reads=[b_O[j], b_Sp[j]], writes=[pb])
                            i_mm += 1
                    if d == 0:
                        S.op("act", lambda e: e.activation(out=y_[:, TL + tau:T:CH], in_=ps[:, 0:TC // CH], func=AF.Copy), reads=[pb], writes=[by])
                        S.op("act", lambda e: e.activation(out=y_[:, tau:TL:CH], in_=ps[:, TC // CH:NCK], func=AF.Copy), reads=[pb], writes=[by])
                    else:
                        S.op("act", lambda e: e.activation(out=y_[:, T - 1 - tau::-CH], in_=ps[:, 0:NCK], func=AF.Copy), reads=[pb], writes=[by])
                if d == 0:
                    S.dma("sp", self.ysT[q * 128:(q + 1) * 128, :], y_[:], reads=[by], writes=[b_ys[q]], owner=by)
                else:
                    for h0 in (0, T // 2):
                        S.dma("pool", self.ysT[q * 128:(q + 1) * 128, h0:h0 + T // 2], y_[:, h0:h0 + T // 2], reads=[by], writes=[b_ys[q]], owner=by, accum_op=ALU.add)
        S.barrier()


Prog.stage_ssm = _ssm3_stage
```
